# Optimizing a Trainium2 kernel written in Bass

```python
import jax, jax.numpy as jnp
from jax import lax
import numpy as np

D_MODEL = 1024
BATCH = 32
SEQ = 2048
DEPTH = 1

GRID_W = 64
CTX_LEN = 256
D_FF = 2816
D_CONV = 1024
CONV_WIDTH = 31
GLA_HEADS = 4
GLA_DK = 128
GLA_DV = 256
GLA_LOWRANK = 16
GLA_TAU = 16.0
GLA_CHUNK = 64
N_MOD = 9
EPS = 1e-6
QK_W = GLA_HEADS * GLA_DK
V_W = GLA_HEADS * GLA_DV
IN_SPLITS = (2 * D_CONV, QK_W, QK_W, V_W, V_W, GLA_LOWRANK, GLA_LOWRANK, D_MODEL, D_MODEL)
D_IN = 2 * D_CONV + 2 * QK_W + 2 * V_W + 2 * GLA_LOWRANK + 2 * D_MODEL

kernel_name = 'hybrid_conv_gla_macaron_dit_layer'


def rmsnorm(h, g):
    hf = h.astype(jnp.float32)
    y = hf * lax.rsqrt(jnp.mean(hf * hf, axis=-1, keepdims=True) + EPS)
    return (y * g.astype(jnp.float32)).astype(h.dtype)


def layernorm(h, g, b):
    hf = h.astype(jnp.float32)
    mu = jnp.mean(hf, axis=-1, keepdims=True)
    var = jnp.mean(jnp.square(hf - mu), axis=-1, keepdims=True)
    y = (hf - mu) * lax.rsqrt(var + EPS)
    return (y * g.astype(jnp.float32) + b.astype(jnp.float32)).astype(h.dtype)


def modulate(h, g, shift, scale):
    return rmsnorm(h, g) * (1 + scale) + shift


def half_ffn(h, g, shift, scale, gate, w_gu, w_down):
    u = modulate(h, g, shift, scale)
    a, b = jnp.split(u @ w_gu, 2, axis=-1)
    return h + 0.5 * gate * ((jax.nn.silu(a) * b) @ w_down)


def split_in(p):
    offs, o = [], 0
    for w in IN_SPLITS[:-1]:
        o += w
        offs.append(o)
    return jnp.split(p, offs, axis=-1)


def heads(t, d):
    return t.reshape(t.shape[:-1] + (GLA_HEADS, d))


def flip(t):
    return jnp.flip(t, axis=1)


def log_decay(lr, w, b):
    z = (lr @ w + b).astype(jnp.float32)
    return heads(jax.nn.log_sigmoid(z) / GLA_TAU, GLA_DK)


def conformer_conv(p, dw_w, dw_b, ln_g, ln_b, w_o):
    a, b = jnp.split(p, 2, axis=-1)
    z = a * jax.nn.sigmoid(b)
    z = lax.conv_general_dilated(
        z, dw_w[:, None, :].astype(z.dtype), window_strides=(1,),
        padding=((CONV_WIDTH // 2, CONV_WIDTH // 2),),
        dimension_numbers=('NWC', 'WIO', 'NWC'), feature_group_count=D_CONV) + dw_b
    z = jax.nn.silu(layernorm(z, ln_g, ln_b))
    return z @ w_o


def gla_scan(q, k, v, log_a, s0):
    bsz, L = q.shape[0], q.shape[1]
    n = L // GLA_CHUNK

    def chunks(t):
        t = t.astype(jnp.float32).reshape(bsz, n, GLA_CHUNK, GLA_HEADS, t.shape[-1])
        return jnp.transpose(t, (1, 0, 3, 2, 4))

    mask = jnp.tril(jnp.ones((GLA_CHUNK, GLA_CHUNK), dtype=bool))[:, :, None]

    def step(s, inp):
        qc, kc, vc, gc = inp
        b = jnp.cumsum(gc, axis=2)
        inter = jnp.einsum('bhtk,bhkv->bhtv', qc * jnp.exp(b), s)
        rel = jnp.where(mask, b[:, :, :, None, :] - b[:, :, None, :, :], -jnp.inf)
        att = jnp.einsum('bhtk,bhtsk,bhsk->bhts', qc, jnp.exp(rel), kc)
        intra = jnp.einsum('bhts,bhsv->bhtv', att, vc)
        b_last = b[:, :, -1:, :]
        s_new = jnp.exp(b_last[:, :, 0, :, None]) * s + jnp.einsum(
            'bhsk,bhsv->bhkv', kc * jnp.exp(b_last - b), vc)
        return s_new, inter + intra

    _, o = lax.scan(step, s0, (chunks(q), chunks(k), chunks(v), chunks(log_a)))
    return jnp.transpose(o, (1, 0, 3, 2, 4)).reshape(bsz, L, GLA_HEADS, GLA_DV)


def gla_final_state(k, v, log_a):
    b = jnp.cumsum(log_a.astype(jnp.float32), axis=1)
    w = jnp.exp(b[:, -1:] - b)
    return jnp.einsum('blhk,blhv->bhkv', k.astype(jnp.float32) * w, v.astype(jnp.float32))


def bidir_gla(q, k, v, la_f, la_b, s_f, s_b):
    o_f = gla_scan(q, k, v, la_f, s_f)
    o_b = gla_scan(flip(q), flip(k), flip(v), flip(la_b), s_b)
    return o_f + flip(o_b)


def gla_output(o, og, gn_g, w_go):
    o = o * lax.rsqrt(jnp.mean(o * o, axis=-1, keepdims=True) + EPS)
    o = (o.reshape(o.shape[0], o.shape[1], V_W) * gn_g.astype(jnp.float32)).astype(og.dtype)
    return (o * jax.nn.silu(og)) @ w_go


def branch_merge(conv_in, o, og, ga, gb, dw_w, dw_b, ln_g, ln_b, w_co, gn_g, w_go, w_o):
    y_conv = conformer_conv(conv_in, dw_w, dw_b, ln_g, ln_b, w_co)
    y_gla = gla_output(o, og, gn_g, w_go)
    return (jax.nn.sigmoid(ga) * y_conv + jax.nn.sigmoid(gb) * y_gla) @ w_o


def setup_inputs(seed: int = 0) -> dict:
    key = jax.random.key(seed)
    ks = jax.random.split(key, 32)
    f32 = jnp.float32
    D, L = D_MODEL, DEPTH

    def nrm(k, shape, scale):
        return jax.random.normal(k, shape, f32) * scale

    def gain(k, shape):
        return 1.0 + 0.05 * jax.random.normal(k, shape, f32)

    return {
        'x': nrm(ks[0], (BATCH, SEQ, D), 1.0),
        'c': nrm(ks[1], (BATCH, D), 1.0),
        'ctx': nrm(ks[2], (BATCH, CTX_LEN, D), 1.0),
        'c_ctx': nrm(ks[3], (D,), 1.0),
        'w_mod': nrm(ks[4], (L, D, N_MOD * D), 0.5 * D ** -0.5),
        'b_mod': nrm(ks[5], (L, N_MOD * D), 0.01),
        'g_ffn1': gain(ks[6], (L, D)),
        'w1_gu': nrm(ks[7], (L, D, 2 * D_FF), D ** -0.5),
        'w1_down': nrm(ks[8], (L, D_FF, D), D_FF ** -0.5),
        'g_mix': gain(ks[9], (L, D)),
        'w_in': nrm(ks[10], (L, D, D_IN), D ** -0.5),
        'dw_weight': nrm(ks[11], (L, CONV_WIDTH, D_CONV), CONV_WIDTH ** -0.5),
        'dw_bias': nrm(ks[12], (L, D_CONV), 0.01),
        'conv_ln_g': gain(ks[13], (L, D_CONV)),
        'conv_ln_b': nrm(ks[14], (L, D_CONV), 0.01),
        'w_conv_out': nrm(ks[15], (L, D_CONV, D), D_CONV ** -0.5),
        'w_alpha_f': nrm(ks[16], (L, GLA_LOWRANK, QK_W), GLA_LOWRANK ** -0.5),
        'b_alpha_f': nrm(ks[17], (L, QK_W), 0.1),
        'w_alpha_b': nrm(ks[18], (L, GLA_LOWRANK, QK_W), GLA_LOWRANK ** -0.5),
        'b_alpha_b': nrm(ks[19], (L, QK_W), 0.1),
        'gla_norm_g': gain(ks[20], (L, V_W)),
        'w_gla_out': nrm(ks[21], (L, V_W, D), V_W ** -0.5),
        'w_out': nrm(ks[22], (L, D, D), D ** -0.5),
        'g_ffn2': gain(ks[23], (L, D)),
        'w2_gu': nrm(ks[24], (L, D, 2 * D_FF), D ** -0.5),
        'w2_down': nrm(ks[25], (L, D_FF, D), D_FF ** -0.5),
        'g_final': gain(ks[26], (D,)),
    }


def reference(x, c, ctx, c_ctx, w_mod, b_mod, g_ffn1, w1_gu, w1_down, g_mix, w_in,
              dw_weight, dw_bias, conv_ln_g, conv_ln_b, w_conv_out, w_alpha_f, b_alpha_f,
              w_alpha_b, b_alpha_b, gla_norm_g, w_gla_out, w_out, g_ffn2, w2_gu, w2_down, g_final):
    q_scale = GLA_DK ** -0.5
    h = ctx
    for l in range(DEPTH):
        last = l == DEPTH - 1
        mx = jnp.split((jax.nn.silu(c) @ w_mod[l] + b_mod[l])[:, None, :], N_MOD, axis=-1)
        mc = jnp.split(jax.nn.silu(c_ctx) @ w_mod[l] + b_mod[l], N_MOD, axis=-1)

        x = half_ffn(x, g_ffn1[l], mx[0], mx[1], mx[2], w1_gu[l], w1_down[l])
        h = half_ffn(h, g_ffn1[l], mc[0], mc[1], mc[2], w1_gu[l], w1_down[l])

        conv_x, q_x, k_x, v_x, og_x, af_x, ab_x, ga_x, gb_x = split_in(
            modulate(x, g_mix[l], mx[3], mx[4]) @ w_in[l])
        conv_c, q_c, k_c, v_c, og_c, af_c, ab_c, ga_c, gb_c = split_in(
            modulate(h, g_mix[l], mc[3], mc[4]) @ w_in[l])

        k_c, v_c = heads(k_c, GLA_DK), heads(v_c, GLA_DV)
        laf_c = log_decay(af_c, w_alpha_f[l], b_alpha_f[l])
        lab_c = log_decay(ab_c, w_alpha_b[l], b_alpha_b[l])
        s_f = gla_final_state(k_c, v_c, laf_c)
        s_b = gla_final_state(flip(k_c), flip(v_c), flip(lab_c))

        o_x = bidir_gla(heads(q_x, GLA_DK) * q_scale, heads(k_x, GLA_DK), heads(v_x, GLA_DV),
                        log_decay(af_x, w_alpha_f[l], b_alpha_f[l]),
                        log_decay(ab_x, w_alpha_b[l], b_alpha_b[l]), s_f, s_b)
        mix_x = branch_merge(conv_x, o_x, og_x, ga_x, gb_x, dw_weight[l], dw_bias[l], conv_ln_g[l],
                             conv_ln_b[l], w_conv_out[l], gla_norm_g[l], w_gla_out[l], w_out[l])

        if not last:
            zeros = jnp.zeros_like(s_f)
            o_c = bidir_gla(heads(q_c, GLA_DK) * q_scale, k_c, v_c, laf_c, lab_c, zeros, zeros)
            mix_c = branch_merge(conv_c, o_c, og_c, ga_c, gb_c, dw_weight[l], dw_bias[l], conv_ln_g[l],
                                 conv_ln_b[l], w_conv_out[l], gla_norm_g[l], w_gla_out[l], w_out[l])
            h = h + mc[5] * mix_c
            h = half_ffn(h, g_ffn2[l], mc[6], mc[7], mc[8], w2_gu[l], w2_down[l])

        x = x + mx[5] * mix_x
        x = half_ffn(x, g_ffn2[l], mx[6], mx[7], mx[8], w2_gu[l], w2_down[l])
    return rmsnorm(x, g_final)
```

```python
import numpy as np
from contextlib import ExitStack
import concourse.bass as bass
import concourse.mybir as mybir
from concourse.bass_utils import run_bass_kernel_spmd

F32 = mybir.dt.float32
BF16 = mybir.dt.bfloat16
AF = mybir.ActivationFunctionType
ALU = mybir.AluOpType

ENGS = ("pe", "act", "dve", "pool", "sp")
NDMASEM = 8

D = 1024
DFF = 2816
NJ = DFF // 128
DIN = 7200
NMOD = 9
EPS = 1e-6
NCORES = 8


class T:
    __slots__ = ("name", "w", "r", "psum")

    def __init__(self, name, psum=False):
        self.name = name
        self.w = None
        self.r = []
        self.psum = psum


class Op:
    __slots__ = ("eng", "fn", "idx", "waits", "inc", "dma", "sem", "semval", "vc", "cnt")

    def __init__(self, eng, fn, idx, dma):
        self.eng = eng
        self.fn = fn
        self.idx = idx
        self.dma = dma
        self.waits = []
        self.inc = False
        self.sem = None
        self.semval = 0
        self.vc = None
        self.cnt = 0


class Prog:
    def __init__(self):
        self.ops = {e: [] for e in ENGS}
        self.clock = {e: {} for e in ENGS}
        self.ndma = {e: 0 for e in ENGS}
        self.dma_ops = {e: [] for e in ENGS}
        self._prepared = False

    def add(self, eng, fn, reads=(), writes=(), dma=False):
        op = Op(eng, fn, len(self.ops[eng]), dma)
        clk = self.clock[eng]
        deps = []

        def need(o, kind):
            if o is None:
                return
            if o.eng == eng and not o.dma and not dma and eng != "pool":
                if eng == "pe" or kind == "waw":
                    return
            deps.append(o)

        for t in reads:
            need(t.w, "raw")
            if t.psum:
                for o in t.r:
                    if o.eng != eng:
                        need(o, "war")
        for t in writes:
            need(t.w, "waw")
            for o in t.r:
                need(o, "war")
        if dma:
            k = self.ndma[eng]
            self.ndma[eng] = k + 1
            op.sem = k % NDMASEM
            op.semval = 16 * (k // NDMASEM + 1)
            if k >= NDMASEM:
                deps.append(self.dma_ops[eng][k - NDMASEM])
            self.dma_ops[eng].append(op)
        best = {}
        for o in deps:
            key = (o.eng, "d", o.sem) if o.dma else (o.eng, "c")
            val = o.semval if o.dma else o.idx
            if key not in best or best[key][0] < val:
                best[key] = (val, o)
        for key, (val, o) in best.items():
            if clk.get(key, -1) >= val:
                continue
            op.waits.append(o)
            o.inc = True
            for k2, v2 in o.vc.items():
                if clk.get(k2, -1) < v2:
                    clk[k2] = v2
            clk[key] = val
        op.vc = dict(clk)
        for t in writes:
            t.w = op
            t.r = []
        for t in reads:
            t.r.append(op)
        self.ops[eng].append(op)
        return op

    def prepare(self):
        for e in ENGS:
            c = 0
            for o in self.ops[e]:
                if not o.dma and o.inc:
                    c += 1
                    o.cnt = c

    def emit_one(self, e, eng, sems, dsems):
        if not self._prepared:
            self.prepare()
            self._prepared = True
        for o in self.ops[e]:
            best = {}
            for d in o.waits:
                if d.dma:
                    key = ("d", d.eng, d.sem)
                    v = d.semval
                else:
                    key = ("c", d.eng)
                    v = d.cnt
                if best.get(key, -1) < v:
                    best[key] = v
            for key, v in best.items():
                if key[0] == "d":
                    eng.wait_ge(dsems[key[1]][key[2]], v)
                else:
                    eng.wait_ge(sems[key[1]], v)
            ins = o.fn(eng)
            if ins is None:
                continue
            if o.dma:
                ins.then_inc(dsems[e][o.sem], 16)
            elif o.inc:
                ins.then_inc(sems[e], 1)


class Rot:
    def __init__(self, items):
        self.items = items
        self.i = 0

    def next(self):
        it = self.items[self.i % len(self.items)]
        self.i += 1
        return it


def build(nseq, NT, NCX):
    assert NT % 512 == 0 and NCX % 128 == 0 and NCX <= 512
    NTT = NT // 512
    NXB = NT // 128
    NCB = NCX // 128
    TOK = NT + NCX
    NB = TOK // 128
    NCOL = nseq + 1
    nc = bass.Bass("TRN2", target_bir_lowering=False)

    def dram_in(name, shape, dt=F32):
        return nc.dram_tensor(name, shape, dt, kind="ExternalInput").ap()

    def dram_sc(name, shape, dt):
        return nc.dram_tensor(name, shape, dt, kind="Internal").ap()

    x_d = dram_in("x", [nseq * NT, D])
    ctx_d = dram_in("ctx", [nseq * NCX, D])
    cT_d = dram_in("cT", [128, 8, NCOL])
    vecs_d = dram_in("vecs", [128, 384])
    wal_d = dram_in("walpha", [64, 512])
    cst_d = dram_in("consts", [128, 7, 128])
    wmod_d = dram_in("w_mod", [D, NMOD * D])
    w1gu_d = dram_in("w1_gu", [D, 2 * DFF])
    w1d_d = dram_in("w1_down", [DFF, D])
    win_d = dram_in("w_in", [D, DIN])
    wco_d = dram_in("w_conv_out", [D, D])
    wgo_d = dram_in("w_gla_out", [D, D])
    wout_d = dram_in("w_out", [D, D])
    w2gu_d = dram_in("w2_gu", [D, 2 * DFF])
    w2d_d = dram_in("w2_down", [DFF, D])
    out_d = nc.dram_tensor("out", [nseq * NT, D], F32, kind="ExternalOutput").ap()

    gu_s = [dram_sc(f"gu_s{i}", [NJ, 128, 8, 256], BF16) for i in range(2)]
    dn_s = [dram_sc(f"dn_s{i}", [8, 128, NJ, 128], BF16) for i in range(2)]
    wf_s = dram_sc("wf_s", [48, 128, 8, 128], BF16)
    wkv_s = dram_sc("wkv_s", [2, 128, 8, 768], BF16)
    wa_s = dram_sc("wa_s", [128, 8, 32], BF16)
    sq_s = [dram_sc(f"sq_s{i}", [8, 128, 8, 128], BF16) for i in range(3)]
    x1_s = dram_sc("x1_s", [nseq, D, NT], F32)
    og_s = dram_sc("og_s", [nseq, D, NT], BF16)
    cv_s = dram_sc("cv_s", [nseq, D, NT], BF16)

    P = Prog()
    es = ExitStack()
    with es:
        cur = [16512]
        SB_END = 229376

        def nbytes(shape, dt):
            n = 1
            for v in shape[1:]:
                n *= v
            return ((n * (4 if dt == F32 else 2) + 63) // 64) * 64

        def sb(name, shape, dt, at=None):
            nb = nbytes(shape, dt)
            if at is None:
                off = cur[0]
                cur[0] += nb
                assert cur[0] <= SB_END, f"SBUF overflow at {name}: {cur[0]}"
            else:
                off = at[0]
                at[0] += nb
            return nc.alloc_sbuf_tensor_at(name, shape, dt, offset=off)

        def psum(name):
            return es.enter_context(nc.psum_tensor(name, [128, 512], F32))

        def rot(name, n, shape, dt, at=None):
            return Rot([(sb(f"{name}{i}", shape, dt, at), T(f"{name}{i}")) for i in range(n)])

        def rot_ts(r):
            return [t for (_, t) in r.items]

        cst = sb("cst", [128, 7, 128], F32); t_cst = T("cst")
        ident = cst[:, 0, :]
        M_le, M_ge, M_gt, M_lt = cst[:, 1, :], cst[:, 3, :], cst[:, 5, :], cst[:, 6, :]
        M_le2, M_ge2 = cst[:, 1:3, :], cst[:, 3:5, :]
        ident_bf = sb("ident_bf", [128, 128], BF16); t_identbf = T("identbf")
        ones_d = sb("ones_d", [128, 128], F32)
        ones_h = sb("ones_h", [128, 128], BF16)
        ones_bf = sb("ones_bf", [128, 128], BF16)
        t_ones = T("ones")
        kc_ = sb("kconst", [128, 2], F32); t_kc = T("kconst")
        vecs = sb("vecs", [128, 384], F32); t_vecs = T("vecs")
        wal = sb("wal", [64, 512], F32); t_wal = T("wal")
        modT = sb("modT", [128, 72, NCOL], F32); t_modT = T("modT")
        SC = sb("SC", [128, NCOL, 9, 8], F32); t_SC = T("SC")
        scT = sb("scT", [128, 8, NCOL], F32); t_scT = T("scT")
        V_GF1, V_GMIX, V_GF2, V_GFIN, V_DWB, V_LNG, V_LNB, V_GN, V_BMOD, V_DW = 0, 8, 16, 24, 32, 40, 48, 56, 64, 136

        pbank = [psum(f"pb{i}") for i in range(8)]
        t_pb = [T(f"pb{i}", psum=True) for i in range(8)]
        PA = Rot([(pbank[0], t_pb[0]), (pbank[1], t_pb[1])])
        PB = Rot([(pbank[2], t_pb[2]), (pbank[3], t_pb[3])])
        PD = Rot([(pbank[4], t_pb[4]), (pbank[5], t_pb[5])])
        PS_ = (pbank[6], t_pb[6])
        PT = (pbank[7], t_pb[7])

        U2T = sb("U2T", [128, 8, TOK], BF16)
        t_U2 = [T(f"U2_{i}") for i in range(NTT + 1)]
        def u2_tile_of_block(b):
            return t_U2[b // 4] if b < NXB else t_U2[NTT]

        tf = rot("tf", 4, [128, 512], F32)
        sqr = rot("sqr", 3, [128, 512], BF16)
        tb = rot("tb", 2, [128, 512], BF16)
        rstd_ = sb("rstd", [128, 512], F32); t_rstd = T("rstd")
        mean_ = sb("mean", [128, 512], F32); t_mean = T("mean")
        W2K = rot("w2k", 4, [128, 8, 128], BF16)
        W4K = rot("w4k", 3, [128, 8, 256], BF16)
        WDN = rot("wdn", 2, [128, NJ, 128], BF16)
        WA = sb("wa", [128, 8, 32], BF16); t_WA = T("wa")

        arena0 = cur[0]
        p = [arena0]
        xtok = sb("xtok", [128, 4, D], F32, p); t_xtok = T("xtok")
        XTs = [sb(f"XT{k}", [128, 8, 512], F32, p) for k in range(2)]; t_XTs = [T(f"XT{k}") for k in range(2)]
        uF = sb("uF", [128, 8, 512], BF16, p); t_uF = [T(f"uF{c}") for c in range(8)]
        hF = sb("hF", [128, NJ, 512], BF16, p); t_hF = [T(f"hF{j}") for j in range(NJ)]
        CVt = sb("CVt", [128, 8, 512], BF16, p); t_CVt = T("CVt")
        OGt = sb("OGt", [128, 8, 512], BF16, p); t_OGt = T("OGt")
        NTb = sb("NTb", [128, 8, 512], BF16, p); t_NTb = T("NTb")
        MTt = sb("MTt", [128, 8, 512], BF16, p); t_MTt = T("MTt")
        end_ffn = p[0]
        SET_FFN = [t_xtok] + t_XTs + t_uF + t_hF + [t_CVt, t_OGt, t_NTb, t_MTt]
        p = [arena0]
        lrT = sb("lrT", [64, TOK], F32, p); t_lrT = T("lrT")
        WKV = rot("wkv", 1, [128, 8, 768], BF16, p)
        qT2 = sb("qT2", [128, 2, NT], BF16, p); t_qT = T("qT")
        kT2 = sb("kT2", [128, 2, TOK], BF16, p); t_kT = T("kT")
        kt2 = sb("kt2", [128, NB, 256], BF16, p); t_kt = T("kt")
        vt2 = sb("vt2", [128, NB, 512], BF16, p); t_vt = T("vt")
        OT2 = sb("OT2", [128, 4, NT], BF16, p); t_OT = [T(f"OT{i}") for i in range(NXB)]
        S32 = [sb(f"S32_{d}", [128, 2, 256], F32, p) for d in range(2)]; t_S32 = [T(f"S32_{d}") for d in range(2)]
        Sbf = [rot(f"Sbf{d}_", 2, [128, 2, 256], BF16, p) for d in range(2)]
        g32 = rot("g32", 2, [128, 256], F32, p)
        ge = rot("ge", 2, [128, 256], F32, p)
        e1 = rot("e1", 4, [128, 2, 128], F32, p)
        e2 = rot("e2", 2, [128, 2, 128], F32, p)
        e3 = rot("e3", 2, [128, 256], F32, p)
        qd = rot("qd", 4, [128, 2, 128], BF16, p)
        kd = rot("kd", 2, [128, 2, 128], BF16, p)
        kh = rot("kh", 2, [128, 256], BF16, p)
        am = rot("am", 2, [128, 2, 128], BF16, p)
        end_gla = p[0]
        SET_GLA = [t_lrT, t_qT, t_kT, t_kt, t_vt] + t_OT + t_S32 + rot_ts(WKV)
        for r_ in (Sbf[0], Sbf[1], g32, ge, e1, e2, e3, qd, kd, kh, am):
            SET_GLA += rot_ts(r_)
        p = [arena0]
        ZT = rot("ZT", 2, [128, NT + 30], BF16, p)
        DG = rot("DG", 2, [128, 31, 128], BF16, p)
        CVc = rot("CVc", 2, [128, NT], BF16, p)
        end_conv = p[0]
        SET_CONV = rot_ts(ZT) + rot_ts(DG) + rot_ts(CVc)
        cur[0] = max(end_ffn, end_gla, end_conv)
        assert cur[0] <= SB_END, f"SBUF overflow: {cur[0]}"
        build.sbuf_report = dict(arena0=arena0, end_ffn=end_ffn, end_gla=end_gla, end_conv=end_conv, top=cur[0], limit=SB_END)

        def phase_alias(new_ts, old_ts):
            ops_ = []
            seen = set()
            for t in old_ts:
                for o in ([t.w] if t.w is not None else []) + t.r:
                    if id(o) not in seen:
                        seen.add(id(o))
                        ops_.append(o)
            for t in new_ts:
                t.r.extend(ops_)

        sems = {e: es.enter_context(nc.semaphore(f"s_{e}")) for e in ENGS}
        dsems = {e: [es.enter_context(nc.semaphore(f"d_{e}{i}")) for i in range(NDMASEM)] for e in ENGS}

        def dma(q, out, in_, reads=(), writes=(), **kw):
            return P.add(q, lambda e: e.dma_start(out=out, in_=in_, **kw), reads=reads, writes=writes, dma=True)

        def mm(out, lhsT, rhs, start, stop, reads, writes):
            return P.add("pe", lambda e: e.matmul(out, lhsT, rhs, start=start, stop=stop), reads=reads, writes=writes)

        def act(out, in_, func, reads, writes, bias=None, scale=None):
            kw = {}
            if bias is not None:
                kw["bias"] = bias
            if scale is not None:
                kw["scale"] = scale
            return P.add("act", lambda e: e.activation(out, in_, func, **kw), reads=reads, writes=writes)

        def tt(out, in0, in1, op, reads, writes, eng="dve"):
            return P.add(eng, lambda e: e.tensor_tensor(out, in0, in1, op), reads=reads, writes=writes)

        def stt(out, in0, scalar, in1, op0, op1, reads, writes):
            return P.add("dve", lambda e: e.scalar_tensor_tensor(out=out, in0=in0, scalar=scalar, in1=in1, op0=op0, op1=op1),
                         reads=reads, writes=writes)

        dma("sp", cst[:], cst_d, writes=[t_cst])
        dma("sp", vecs[:], vecs_d, writes=[t_vecs])
        dma("sp", wal[:], wal_d, writes=[t_wal])
        dma("sp", scT[:], cT_d, writes=[t_scT])
        P.add("dve", lambda e: e.memset(ones_d[:], 1.0 / D), writes=[t_ones])
        P.add("dve", lambda e: e.memset(ones_h[:], 1.0 / 256), writes=[t_ones])
        P.add("dve", lambda e: e.memset(ones_bf[:], 1.0 / D), writes=[t_ones])
        P.add("dve", lambda e: e.memset(kc_[:, 0:1], EPS), writes=[t_kc])
        P.add("dve", lambda e: e.memset(kc_[:, 1:2], 1.0), writes=[t_kc])
        P.add("dve", lambda e: e.tensor_copy(ident_bf[:], ident), reads=[t_cst], writes=[t_identbf])
        eps_ap = kc_[:, 0:1]
        one_ap = kc_[:, 1:2]

        t_gu = [[T(f"gu{i}_{j}") for j in range(NJ)] for i in range(2)]
        t_dn = [[T(f"dn{i}_{o}") for o in range(8)] for i in range(2)]
        t_wf = [T(f"wf{i}") for i in range(48)]
        t_wkv = [T(f"wkv{h}") for h in range(2)]
        t_was = T("was")
        t_sq = [[T(f"sq{i}_{o}") for o in range(8)] for i in range(3)]

        def cast_gu2(i, src):
            for j in range(NJ):
                ta = T("tmp")
                dma("pool", gu_s[i][j, :, :, 0:128], src[:, j * 128:(j + 1) * 128].rearrange("(kc p) c -> p kc c", p=128), writes=[ta])
                dma("pool", gu_s[i][j, :, :, 128:256], src[:, DFF + j * 128:DFF + (j + 1) * 128].rearrange("(kc p) c -> p kc c", p=128),
                    reads=[ta], writes=[t_gu[i][j]])

        def cast_dn(i, src):
            for o in range(8):
                dma("pool", dn_s[i][o], src[:, o * 128:(o + 1) * 128].rearrange("(j p) c -> p j c", p=128), writes=[t_dn[i][o]])

        def cast_sq(i, src):
            for o in range(8):
                dma("pool", sq_s[i][o], src[:, o * 128:(o + 1) * 128].rearrange("(kc p) c -> p kc c", p=128), writes=[t_sq[i][o]])

        def cast_wf(idx, col0):
            dma("pool", wf_s[idx], win_d[:, col0:col0 + 128].rearrange("(kc p) c -> p kc c", p=128), writes=[t_wf[idx]])

        cast_gu2(0, w1gu_d)
        cast_dn(0, w1d_d)
        dma("pool", wa_s, win_d[:, 5120:5152].rearrange("(kc p) c -> p kc c", p=128), writes=[t_was])
        for h in range(4):
            cast_wf(16 + h, 2048 + h * 128)
            cast_wf(20 + h, 2560 + h * 128)
            if h % 2 == 0:
                hp_ = h // 2
                ta = T("tmp")
                dma("pool", wkv_s[hp_, :, :, 0:256], win_d[:, 2560 + hp_ * 256:2560 + (hp_ + 1) * 256].rearrange("(kc p) c -> p kc c", p=128), writes=[ta])
                dma("pool", wkv_s[hp_, :, :, 256:768], win_d[:, 3072 + hp_ * 512:3072 + (hp_ + 1) * 512].rearrange("(kc p) c -> p kc c", p=128),
                    reads=[ta], writes=[t_wkv[hp_]])
            cast_wf(24 + 2 * h, 4096 + (2 * h) * 128)
            cast_wf(24 + 2 * h + 1, 4096 + (2 * h + 1) * 128)
        def late_casts():
            for c in range(8):
                cast_wf(c, c * 128)
                cast_wf(8 + c, 1024 + c * 128)
            cast_sq(0, wco_d)
            cast_sq(1, wgo_d)
            for c in range(8):
                cast_wf(32 + c, 5152 + c * 128)
                cast_wf(40 + c, 6176 + c * 128)
            cast_sq(2, wout_d)
            cast_gu2(1, w2gu_d)
            cast_dn(1, w2d_d)

        P.add("act", lambda e: e.activation(scT[:], scT[:], AF.Silu), reads=[t_scT], writes=[t_scT])
        wm_bufs = [(xtok[:].rearrange("p a (b c) -> p (a b) c", c=512), t_xtok), (XTs[0][:], t_XTs[0]), (XTs[1][:], t_XTs[1])]
        pmod, t_pmod = PS_
        for ch in range(18):
            wm_view, t_wm = wm_bufs[ch % 3]
            dma("sp", wm_view, wmod_d[:, ch * 512:(ch + 1) * 512].rearrange("(kc p) c -> p kc c", p=128), writes=[t_wm])
            pr, t_pr = PA.next()
            for kc in range(8):
                mm(pr[0:NCOL, :], scT[:, kc, :], wm_view[:, kc, :], kc == 0, kc == 7, [t_wm, t_scT], [t_pr])
            rowt, t_rowt = tf.next()
            act(rowt[0:NCOL, :], pr[0:NCOL, :], AF.Copy, [t_pr], [t_rowt])
            for sub in range(4):
                j = ch * 4 + sub
                P.add("pe", lambda e, j=j, sub=sub, rowt=rowt: e.transpose(pmod[:, j * NCOL:(j + 1) * NCOL], rowt[0:NCOL, sub * 128:(sub + 1) * 128],
                                                                         ident[0:NCOL, 0:NCOL]), reads=[t_rowt, t_cst], writes=[t_pmod])
        for col in range(NCOL):
            P.add("dve", lambda e, col=col: e.tensor_tensor(
                modT[:, :, col], pmod[:, 0:72 * NCOL].rearrange("p (j c) -> p j c", c=NCOL)[:, :, col],
                vecs[:, V_BMOD:V_BMOD + 72], ALU.add), reads=[t_pmod, t_vecs], writes=[t_modT])
        for col in range(NCOL):
            for n, gv in enumerate((V_GF1, V_GMIX, V_GF2)):
                sh = modT[:, (3 * n) * 8:(3 * n) * 8 + 8, col]
                scl = modT[:, (3 * n + 1) * 8:(3 * n + 1) * 8 + 8, col]
                gt = modT[:, (3 * n + 2) * 8:(3 * n + 2) * 8 + 8, col]
                P.add("dve", lambda e, col=col, n=n, scl=scl, gv=gv: e.scalar_tensor_tensor(
                    out=SC[:, col, 3 * n, :], in0=scl, scalar=1.0, in1=vecs[:, gv:gv + 8], op0=ALU.add, op1=ALU.mult),
                    reads=[t_modT, t_vecs], writes=[t_SC])
                P.add("dve", lambda e, col=col, n=n, sh=sh: e.tensor_copy(SC[:, col, 3 * n + 1, :], sh), reads=[t_modT], writes=[t_SC])
                P.add("dve", lambda e, col=col, n=n, gt=gt: e.tensor_scalar(
                    SC[:, col, 3 * n + 2, :], gt, (1.0 if n == 1 else 0.5), None, ALU.mult), reads=[t_modT], writes=[t_SC])

        from collections import deque
        qA = deque()
        qB = deque()

        def drain(q, k):
            for _ in range(k):
                if q:
                    q.popleft()()

        def flush(q):
            while q:
                q.popleft()()

        def stats_cl(src3, t_src, W, ones_ap, nchunk=8):
            pst, t_pst = PS_
            sqs = {}

            def sqf(c):
                sq, t_sqq = sqr.next()
                act(sq[:, :W], src3[:, c, :], AF.Square, t_src, [t_sqq])
                sqs[c] = (sq, t_sqq)

            def mk(c):
                def f():
                    if c == 0:
                        sqf(0)
                    if c + 1 < nchunk:
                        sqf(c + 1)
                    sq, t_sqq = sqs.pop(c)
                    mm(pst[:, :W], ones_ap, sq[:, :W], c == 0, c == nchunk - 1, [t_ones, t_sqq], [t_pst])
                return f

            def fin():
                tmp, t_tmp = tf.next()
                act(tmp[:, :W], pst[:, :W], AF.Sqrt, [t_pst, t_kc], [t_tmp], bias=eps_ap)
                P.add("dve", lambda e: e.reciprocal(rstd_[:, :W], tmp[:, :W]), reads=[t_tmp], writes=[t_rstd])
            return [mk(c) for c in range(nchunk)] + [fin]

        def modulate_cl(dst3, t_dst, src3, t_src, W, A, Bv):
            def mk(c):
                def f():
                    tmp, t_tmp = tf.next()
                    stt(tmp[:, :W], src3[:, c, :], A[:, c:c + 1], rstd_[:, :W], ALU.mult, ALU.mult, t_src + [t_SC, t_rstd], [t_tmp])
                    act(dst3[:, c, :], tmp[:, :W], AF.Identity, [t_tmp, t_SC], [t_dst[c] if isinstance(t_dst, list) else t_dst],
                        bias=Bv[:, c:c + 1])
                return f
            return [mk(c) for c in range(8)]

        def gate_up(W, wi, q=None, per=1):
            for j in range(NJ):
                wb, t_wb = W4K.next()
                dma("sp", wb[:], gu_s[wi][j], reads=[t_gu[wi][j]], writes=[t_wb])
                pa, t_pa = PA.next()
                pb, t_pbb = PB.next()
                for kc in range(8):
                    mm(pa[:, :W], wb[:, kc, 0:128], uF[:, kc, :W], kc == 0, kc == 7, [t_wb, t_uF[kc]], [t_pa])
                if q is not None:
                    drain(q, per)
                for kc in range(8):
                    mm(pb[:, :W], wb[:, kc, 128:256], uF[:, kc, :W], kc == 0, kc == 7, [t_wb, t_uF[kc]], [t_pbb])
                if q is not None:
                    drain(q, per)
                sa, t_sa = tf.next()
                act(sa[:, :W], pa[:, :W], AF.Silu, [t_pa], [t_sa])
                tt(hF[:, j, :W], sa[:, :W], pb[:, :W], ALU.mult, [t_sa, t_pbb], [t_hF[j]])

        def down(W, wi, G, X, t_X, q=None, per=3):
            for o in range(8):
                wb, t_wb = WDN.next()
                dma("sp", wb[:], dn_s[wi][o], reads=[t_dn[wi][o]], writes=[t_wb])
                pd, t_pd = PD.next()
                for j in range(NJ):
                    mm(pd[:, :W], wb[:, j, :], hF[:, j, :W], j == 0, j == NJ - 1, [t_wb, t_hF[j]], [t_pd])
                stt(X[:, o, :W], pd[:, :W], G[:, o:o + 1], X[:, o, :W], ALU.mult, ALU.add, [t_pd, t_SC, t_X], [t_X])
                if q is not None:
                    drain(q, per)

        def load_tok(src_rows, W):
            nsub = W // 128
            dma("sp", xtok[:, 0:nsub, :], src_rows.rearrange("(i p) d -> p i d", p=128), writes=[t_xtok])

        def transpose_in(W, X, t_X):
            nsub = W // 128
            for c in range(8):
                ps, t_ps = PA.next()
                for i in range(nsub):
                    P.add("pe", lambda e, ps=ps, i=i, c=c: e.transpose(ps[:, i * 128:(i + 1) * 128], xtok[:, i, c * 128:(c + 1) * 128], ident),
                          reads=[t_xtok, t_cst], writes=[t_ps])
                act(X[:, c, :W], ps[:, :W], AF.Copy, [t_ps], [t_X])

        t_out = []
        for s in range(nseq):
            colx, colc = s, nseq
            t_x1 = [T(f"x1_{s}_{i}") for i in range(NTT)]
            t_ogs = [T(f"ogs_{s}_{h}") for h in range(2)]
            t_cvs = [T(f"cvs_{s}_{c}") for c in range(8)]

            def a_info(ti):
                isx = ti < NTT
                W = 512 if isx else NCX
                col = colx if isx else colc
                tok0 = ti * 512 if isx else NT
                src = x_d[s * NT + ti * 512: s * NT + ti * 512 + 512, :] if isx else ctx_d[s * NCX:(s + 1) * NCX, :]
                return isx, W, col, tok0, src

            def a_n1(ti):
                isx, W, col, tok0, src = a_info(ti)
                X, tX = XTs[ti % 2], t_XTs[ti % 2]
                return (stats_cl(X[:, :, :W], [tX], W, ones_bf[:]) +
                        modulate_cl(uF[:, :, :W], t_uF, X[:, :, :W], [tX], W, SC[:, col, 0, :], SC[:, col, 1, :]))

            def a_n2(ti):
                isx, W, col, tok0, src = a_info(ti)
                X, tX = XTs[ti % 2], t_XTs[ti % 2]
                return (stats_cl(X[:, :, :W], [tX], W, ones_bf[:]) +
                        modulate_cl(U2T[:, :, tok0:tok0 + W], t_U2[ti], X[:, :, :W], [tX], W, SC[:, col, 3, :], SC[:, col, 4, :]))

            isx, W, col, tok0, src = a_info(0)
            load_tok(src, W)
            transpose_in(W, XTs[0], t_XTs[0])
            for cl in a_n1(0):
                cl()
            for ti in range(NTT + 1):
                isx, W, col, tok0, src = a_info(ti)
                X, tX = XTs[ti % 2], t_XTs[ti % 2]
                if ti + 1 <= NTT:
                    isx1, W1, col1, tok1, src1 = a_info(ti + 1)
                    load_tok(src1, W1)
                gate_up(W, 0, qB, 1)
                flush(qB)
                if ti + 1 <= NTT:
                    transpose_in(W1, XTs[(ti + 1) % 2], t_XTs[(ti + 1) % 2])
                    qA.extend(a_n1(ti + 1))
                down(W, 0, SC[:, col, 2, :], X, tX, qA, 3)
                flush(qA)
                if isx:
                    dma("act", x1_s[s, :, ti * 512:(ti + 1) * 512].rearrange("(c p) t -> p c t", p=128), X[:], reads=[tX], writes=[t_x1[ti]])
                qB.extend(a_n2(ti))
            flush(qB)

            phase_alias(SET_GLA, SET_FFN)
            if s == 0:
                late_casts()
            P.add("dve", lambda e: e.memset(lrT[:], 1.0), writes=[t_lrT])
            dma("sp", WA[:], wa_s, reads=[t_was], writes=[t_WA])
            tiles = [(i * 512, 512, t_U2[i]) for i in range(NTT)] + [(NT, NCX, t_U2[NTT])]
            for (t0, W, tu) in tiles:
                ps, t_ps = PA.next()
                for half in range(2):
                    for kc in range(8):
                        mm(ps[32 * half:32 * half + 16, :W], WA[:, kc, 16 * half:16 * half + 16], U2T[:, kc, t0:t0 + W], kc == 0, kc == 7,
                           [t_WA, tu], [t_ps])
                for half in range(2):
                    act(lrT[32 * half:32 * half + 16, t0:t0 + W], ps[32 * half:32 * half + 16, :W], AF.Copy, [t_ps], [t_lrT])

            for hp in range(2):
                for hh in range(2):
                    h = 2 * hp + hh
                    wq, t_wq = W2K.next()
                    dma("sp", wq[:], wf_s[16 + h], reads=[t_wf[16 + h]], writes=[t_wq])
                    for i in range(NTT):
                        ps, t_ps = PA.next()
                        for kc in range(8):
                            mm(ps[:, :], wq[:, kc, :], U2T[:, kc, i * 512:(i + 1) * 512], kc == 0, kc == 7, [t_wq, t_U2[i]], [t_ps])
                        P.add("act", lambda e, i=i, ps=ps, hh=hh: e.mul(qT2[:, hh, i * 512:(i + 1) * 512], ps[:, :], float(128 ** -0.5)),
                              reads=[t_ps], writes=[t_qT])
                    wk, t_wk = W2K.next()
                    dma("sp", wk[:], wf_s[20 + h], reads=[t_wf[20 + h]], writes=[t_wk])
                    for (t0, W, tu) in tiles:
                        ps, t_ps = PB.next()
                        for kc in range(8):
                            mm(ps[:, :W], wk[:, kc, :], U2T[:, kc, t0:t0 + W], kc == 0, kc == 7, [t_wk, tu], [t_ps])
                        P.add("dve", lambda e, ps=ps, t0=t0, W=W, hh=hh: e.tensor_copy(kT2[:, hh, t0:t0 + W], ps[:, :W]), reads=[t_ps], writes=[t_kT])
                wv, t_wv = WKV.next()
                dma("sp", wv[:], wkv_s[hp], reads=[t_wkv[hp]], writes=[t_wv])
                for b in range(NB):
                    pk, t_pk = PA.next()
                    pv, t_pv = PD.next()
                    tu = u2_tile_of_block(b)
                    for kc in range(8):
                        mm(pk[:, 0:256], U2T[:, kc, b * 128:(b + 1) * 128], wv[:, kc, 0:256], kc == 0, kc == 7, [t_wv, tu], [t_pk])
                    for kc in range(8):
                        mm(pv[:, :], U2T[:, kc, b * 128:(b + 1) * 128], wv[:, kc, 256:768], kc == 0, kc == 7, [t_wv, tu], [t_pv])
                    P.add("dve", lambda e, pk=pk, b=b: e.tensor_copy(kt2[:, b, :], pk[:, 0:256]), reads=[t_pk], writes=[t_kt])
                    act(vt2[:, b, :], pv[:, :], AF.Copy, [t_pv], [t_vt])

                ctxb = list(range(NXB, NB))
                xb = list(range(NXB))
                ordf = ctxb + xb
                ordb = ctxb[::-1] + xb[::-1]
                slots = []
                for k in range(NB):
                    slots.append((0, ordf[k]))
                    slots.append((1, ordb[k]))
                nsl = len(slots)
                sbcur = []
                for d in range(2):
                    P.add("dve", lambda e, d=d: e.memset(S32[d][:], 0.0), writes=[t_S32[d]])
                    sb0, t_sb0 = Sbf[d].next()
                    P.add("dve", lambda e, sb0=sb0: e.memset(sb0[:], 0.0), writes=[t_sb0])
                    sbcur.append((sb0, t_sb0))
                visited = set()
                ST = [dict() for _ in range(nsl)]
                ZR = [(pbank[0], t_pb[0]), (pbank[1], t_pb[1])]
                BA = [(pbank[2], t_pb[2]), (pbank[3], t_pb[3])]
                SN = [(pbank[4], t_pb[4]), (pbank[5], t_pb[5])]
                OB = [(pbank[6], t_pb[6]), (pbank[7], t_pb[7])]

                def gstage(j, i, hp=hp):
                    d, b = slots[i]
                    st = ST[i]
                    base = 32 * d
                    Mcum = M_le if d == 0 else M_ge
                    Mrem = M_gt if d == 0 else M_lt
                    Mmask2 = M_le2 if d == 0 else M_ge2
                    last = 127 if d == 0 else 0
                    isx = b < NXB
                    tok = slice(b * 128, (b + 1) * 128)
                    zr, t_zr = ZR[i % 2]
                    ba, t_ba = BA[i % 2]
                    sn, t_sn = SN[i % 2]
                    ob, t_ob = OB[i % 2]
                    if j == 0:
                        mm(zr[:, 0:256], lrT[base:base + 17, tok], wal[base:base + 17, hp * 256:(hp + 1) * 256], True, True,
                           [t_lrT, t_wal], [t_zr])
                    elif j == 1:
                        gE, t_gE = ge.next()
                        act(gE[:], zr[:, 0:256], AF.Exp, [t_zr], [t_gE], scale=-1.0)
                        gS, t_gS = g32.next()
                        act(gS[:], gE[:], AF.Ln, [t_gE, t_kc], [t_gS], bias=one_ap)
                        st.update(gS=gS, t_gS=t_gS)
                    elif j == 2:
                        gS, t_gS = st["gS"], st["t_gS"]
                        mm(zr[:, 256:512], Mrem, gS[:], True, True, [t_gS, t_cst], [t_zr])
                        for hh in range(2):
                            mm(ba[:, hh * 128:(hh + 1) * 128], gS[:, hh * 128:(hh + 1) * 128], Mcum, True, True, [t_gS, t_cst], [t_ba])
                    elif j == 3:
                        E1, t_E1 = e1.next()
                        act(E1[:], ba[:, 0:256].rearrange("p (h t) -> p h t", h=2), AF.Exp, [t_ba], [t_E1], scale=-1.0 / 16)
                        E3, t_E3 = e3.next()
                        act(E3[:], zr[:, 256:512], AF.Exp, [t_zr], [t_E3], scale=-1.0 / 16)
                        st.update(E1=E1, t_E1=t_E1, E3=E3, t_E3=t_E3)
                        if isx:
                            E2, t_E2 = e2.next()
                            act(E2[:], ba[:, 0:256].rearrange("p (h t) -> p h t", h=2), AF.Exp, [t_ba], [t_E2], scale=1.0 / 16)
                            st.update(E2=E2, t_E2=t_E2)
                    elif j == 4:
                        E1, t_E1, E3, t_E3 = st["E1"], st["t_E1"], st["E3"], st["t_E3"]
                        KH, t_KH = kh.next()
                        tt(KH[:], kt2[:, b, :], E3[:], ALU.mult, [t_kt, t_E3], [t_KH])
                        st.update(KH=KH, t_KH=t_KH)
                        if isx:
                            E2, t_E2 = st["E2"], st["t_E2"]
                            QD, t_QD = qd.next()
                            qk_eng = "pool" if s > 0 else "dve"
                            tt(QD[:], qT2[:, :, tok], E1[:], ALU.mult, [t_qT, t_E1], [t_QD], eng=qk_eng)
                            KD, t_KD = kd.next()
                            tt(KD[:], kT2[:, :, tok], E2[:], ALU.mult, [t_kT, t_E2], [t_KD], eng=qk_eng)
                            st.update(QD=QD, t_QD=t_QD, KD=KD, t_KD=t_KD)
                    elif j == 5:
                        KH, t_KH = st["KH"], st["t_KH"]
                        for hh in range(2):
                            mm(sn[:, hh * 256:(hh + 1) * 256], KH[:, hh * 128:(hh + 1) * 128], vt2[:, b, hh * 256:(hh + 1) * 256], True, True,
                               [t_KH, t_vt], [t_sn])
                        if isx:
                            QD, t_QD, KD, t_KD = st["QD"], st["t_QD"], st["KD"], st["t_KD"]
                            for hh in range(2):
                                mm(ba[:, 256 + hh * 128:256 + (hh + 1) * 128], KD[:, hh, :], QD[:, hh, :], True, True, [t_KD, t_QD], [t_ba])
                    elif j == 6:
                        E1, t_E1 = st["E1"], st["t_E1"]
                        for hh in range(2):
                            stt(S32[d][:, hh, :], S32[d][:, hh, :], E1[:, hh, last:last + 1], sn[:, hh * 256:(hh + 1) * 256], ALU.mult, ALU.add,
                                [t_S32[d], t_E1, t_sn], [t_S32[d]])
                        if isx:
                            AM, t_AM = am.next()
                            tt(AM[:], ba[:, 256:512].rearrange("p (h t) -> p h t", h=2), Mmask2, ALU.mult, [t_ba, t_cst], [t_AM])
                            st.update(AM=AM, t_AM=t_AM)
                    elif j == 7:
                        if isx:
                            QD, t_QD, AM, t_AM = st["QD"], st["t_QD"], st["AM"], st["t_AM"]
                            sbp, t_sbp = sbcur[d]
                            for hh in range(2):
                                for jj in range(2):
                                    cc = hh * 2 + jj
                                    mm(ob[:, cc * 128:(cc + 1) * 128], vt2[:, b, cc * 128:(cc + 1) * 128], AM[:, hh, :], True, False,
                                       [t_vt, t_AM], [t_ob])
                                    mm(ob[:, cc * 128:(cc + 1) * 128], sbp[:, hh, jj * 128:(jj + 1) * 128], QD[:, hh, :], False, True,
                                       [t_sbp, t_QD], [t_ob])
                        nsb, t_nsb = Sbf[d].next()
                        P.add("act", lambda e, nsb=nsb, d=d: e.copy(nsb[:], S32[d][:]), reads=[t_S32[d]], writes=[t_nsb])
                        sbcur[d] = (nsb, t_nsb)
                    elif j == 8:
                        if isx:
                            src = ob[:, :].rearrange("p (c t) -> p c t", c=4)
                            if b not in visited:
                                visited.add(b)
                                P.add("act", lambda e, src=src, tok=tok: e.copy(OT2[:, :, tok], src), reads=[t_ob], writes=[t_OT[b]])
                            else:
                                P.add("dve", lambda e, src=src, tok=tok: e.tensor_tensor(OT2[:, :, tok], OT2[:, :, tok], src, ALU.add),
                                      reads=[t_ob, t_OT[b]], writes=[t_OT[b]])

                NST = 9
                for step in range(nsl + NST - 1):
                    for j in range(NST - 1, -1, -1):
                        i = step - j
                        if 0 <= i < nsl:
                            gstage(j, i)

                wog = []
                for cc in range(4):
                    wb, t_wb = W2K.next()
                    dma("sp", wb[:], wf_s[24 + 4 * hp + cc], reads=[t_wf[24 + 4 * hp + cc]], writes=[t_wb])
                    wog.append((wb, t_wb))
                for i in range(NTT):
                    ts_ = slice(i * 512, (i + 1) * 512)
                    t_oti = t_OT[4 * i:4 * i + 4]
                    for hh in range(2):
                        pst, t_pst = PS_
                        for jj in range(2):
                            sq, t_sqq = sqr.next()
                            act(sq[:], OT2[:, hh * 2 + jj, ts_], AF.Square, t_oti, [t_sqq])
                            mm(pst[:], ones_h[:], sq[:], jj == 0, jj == 1, [t_ones, t_sqq], [t_pst])
                        tmp, t_tmp = tf.next()
                        act(tmp[:], pst[:], AF.Sqrt, [t_pst, t_kc], [t_tmp], bias=eps_ap)
                        P.add("dve", lambda e, tmp=tmp: e.reciprocal(rstd_[:], tmp[:]), reads=[t_tmp], writes=[t_rstd])
                        for jj in range(2):
                            cc = hh * 2 + jj
                            wb, t_wb = wog[cc]
                            ps, t_ps = PA.next()
                            for kc in range(8):
                                mm(ps[:], wb[:, kc, :], U2T[:, kc, ts_], kc == 0, kc == 7, [t_wb, t_U2[i]], [t_ps])
                            sg, t_sg = tf.next()
                            act(sg[:], ps[:], AF.Silu, [t_ps], [t_sg])
                            t1, t_t1 = tf.next()
                            tt(t1[:], OT2[:, cc, ts_], rstd_[:], ALU.mult, t_oti + [t_rstd], [t_t1])
                            gcol = V_GN + 4 * hp + cc
                            stt(OT2[:, cc, ts_], t1[:], vecs[:, gcol:gcol + 1], sg[:], ALU.mult, ALU.mult,
                                [t_t1, t_vecs, t_sg], t_oti)
                dma("act", og_s[s, hp * 512:(hp + 1) * 512, :].rearrange("(j p) t -> p j t", p=128), OT2[:], reads=t_OT, writes=[t_ogs[hp]])

            phase_alias(SET_CONV, SET_FFN + SET_GLA)
            def conv_proj(c):
                zt, t_zt = ZT.next()
                P.add("dve", lambda e, zt=zt: e.memset(zt[:, 0:15], 0.0), writes=[t_zt])
                P.add("dve", lambda e, zt=zt: e.memset(zt[:, NT + 15:NT + 30], 0.0), writes=[t_zt])
                wa_, t_wa_ = W2K.next()
                dma("sp", wa_[:], wf_s[c], reads=[t_wf[c]], writes=[t_wa_])
                wb_, t_wb_ = W2K.next()
                dma("sp", wb_[:], wf_s[8 + c], reads=[t_wf[8 + c]], writes=[t_wb_])
                for i in range(NTT):
                    ts_ = slice(i * 512, (i + 1) * 512)
                    pa, t_pa = PA.next()
                    pb, t_pbb = PB.next()
                    for kc in range(8):
                        mm(pa[:], wa_[:, kc, :], U2T[:, kc, ts_], kc == 0, kc == 7, [t_wa_, t_U2[i]], [t_pa])
                    for kc in range(8):
                        mm(pb[:], wb_[:, kc, :], U2T[:, kc, ts_], kc == 0, kc == 7, [t_wb_, t_U2[i]], [t_pbb])
                    sg, t_sg = tf.next()
                    act(sg[:], pb[:], AF.Sigmoid, [t_pbb], [t_sg])
                    tt(zt[:, 15 + i * 512:15 + (i + 1) * 512], sg[:], pa[:], ALU.mult, [t_sg, t_pa], [t_zt])
                dg, t_dg = DG.next()
                for tap in range(31):
                    P.add("dve", lambda e, dg=dg, tap=tap, c=c: e.tensor_scalar(
                        dg[:, tap, :], ident_bf[:], vecs[:, V_DW + c * 31 + tap:V_DW + c * 31 + tap + 1], None, ALU.mult),
                        reads=[t_identbf, t_vecs], writes=[t_dg])
                return zt, t_zt, dg, t_dg

            def conv_mm(c, zt, t_zt, dg, t_dg):
                cvc, t_cvc = CVc.next()
                for i in range(NTT):
                    pd, t_pd = PD.next()
                    for tap in range(31):
                        mm(pd[:], dg[:, tap, :], zt[:, i * 512 + tap:i * 512 + tap + 512], tap == 0, tap == 30, [t_dg, t_zt], [t_pd])
                    act(cvc[:, i * 512:(i + 1) * 512], pd[:], AF.Identity, [t_pd, t_vecs], [t_cvc], bias=vecs[:, V_DWB + c:V_DWB + c + 1])
                dma("act", cv_s[s, c * 128:(c + 1) * 128, :], cvc[:], reads=[t_cvc], writes=[t_cvs[c]])

            cur_c = conv_proj(0)
            for c in range(8):
                nxt_c = conv_proj(c + 1) if c + 1 < 8 else None
                conv_mm(c, *cur_c)
                cur_c = nxt_c

            phase_alias(SET_FFN, SET_GLA + SET_CONV)

            def c_m1(i):
                ts_ = slice(i * 512, (i + 1) * 512)
                X, tX = XTs[i % 2], t_XTs[i % 2]
                pst, t_pst = PS_
                cl = []

                dma("sp", CVt[:], cv_s[s, :, ts_].rearrange("(c p) t -> p c t", p=128), reads=t_cvs, writes=[t_CVt])

                def ld_og():
                    dma("sp", OGt[:], og_s[s, :, ts_].rearrange("(c p) t -> p c t", p=128), reads=t_ogs, writes=[t_OGt])

                def ld_x():
                    dma("sp", X[:], x1_s[s, :, ts_].rearrange("(c p) t -> p c t", p=128), reads=[t_x1[i]], writes=[tX])

                def mean_mm(c):
                    def f():
                        mm(pst[:], ones_bf[:], CVt[:, c, :], c == 0, c == 7, [t_ones, t_CVt], [t_pst])
                        if c == 7:
                            P.add("act", lambda e: e.copy(mean_[:], pst[:]), reads=[t_pst], writes=[t_mean])
                    return f
                cl += [mean_mm(c) for c in range(8)]
                cl.append(ld_og)
                sqs = {}

                def sqf(c):
                    sq, t_sqq = tb.next()
                    act(sq[:], CVt[:, c, :], AF.Square, [t_CVt], [t_sqq])
                    sqs[c] = (sq, t_sqq)

                def ex2_mm(c):
                    def f():
                        if c == 0:
                            sqf(0)
                        if c + 1 < 8:
                            sqf(c + 1)
                        sq, t_sqq = sqs.pop(c)
                        mm(pst[:], ones_bf[:], sq[:], c == 0, c == 7, [t_ones, t_sqq], [t_pst])
                    return f
                cl += [ex2_mm(c) for c in range(8)]
                cl.append(ld_x)

                def var_chain():
                    m2, t_m2 = tf.next()
                    tt(m2[:], mean_[:], mean_[:], ALU.mult, [t_mean], [t_m2])
                    var, t_var = tf.next()
                    tt(var[:], pst[:], m2[:], ALU.subtract, [t_pst, t_m2], [t_var])
                    P.add("dve", lambda e, var=var: e.tensor_scalar(var[:], var[:], 0.0, None, ALU.max), reads=[t_var], writes=[t_var])
                    sd, t_sd = tf.next()
                    act(sd[:], var[:], AF.Sqrt, [t_var, t_kc], [t_sd], bias=eps_ap)
                    P.add("dve", lambda e, sd=sd: e.reciprocal(rstd_[:], sd[:]), reads=[t_sd], writes=[t_rstd])
                cl.append(var_chain)

                def nrm(c):
                    def f():
                        d1, t_d1 = tf.next()
                        tt(d1[:], CVt[:, c, :], mean_[:], ALU.subtract, [t_CVt, t_mean], [t_d1])
                        d2, t_d2 = tf.next()
                        tt(d2[:], d1[:], rstd_[:], ALU.mult, [t_d1, t_rstd], [t_d2])
                        act(NTb[:, c, :], d2[:], AF.Silu, [t_d2, t_vecs], [t_NTb], bias=vecs[:, V_LNB + c:V_LNB + c + 1],
                            scale=vecs[:, V_LNG + c:V_LNG + c + 1])
                    return f
                cl += [nrm(c) for c in range(8)]
                return cl

            def c_m7(i):
                X, tX = XTs[i % 2], t_XTs[i % 2]
                cl = stats_cl(X[:, :, :], [tX], 512, ones_bf[:])

                def scale(c):
                    def f():
                        stt(X[:, c, :], X[:, c, :], vecs[:, V_GFIN + c:V_GFIN + c + 1], rstd_[:], ALU.mult, ALU.mult,
                            [tX, t_vecs, t_rstd], [tX])
                    return f
                cl += [scale(c) for c in range(8)]

                def tr(sub, half):
                    def f():
                        ps, t_ps = PD.next()
                        for cc in range(4):
                            c = half * 4 + cc
                            P.add("pe", lambda e, ps=ps, cc=cc, c=c, sub=sub: e.transpose(
                                ps[:, cc * 128:(cc + 1) * 128], X[:, c, sub * 128:(sub + 1) * 128], ident), reads=[tX, t_cst], writes=[t_ps])
                        if half == 0:
                            act(xtok[:, sub, 0:512], ps[:], AF.Copy, [t_ps], [t_xtok])
                        else:
                            P.add("dve", lambda e, ps=ps, sub=sub: e.tensor_copy(xtok[:, sub, 512:1024], ps[:]), reads=[t_ps], writes=[t_xtok])
                    return f
                cl += [tr(sub, half) for sub in range(4) for half in range(2)]

                def store():
                    to = T(f"out_{s}_{i}")
                    r0 = s * NT + i * 512
                    dma("act", out_d[r0:r0 + 512, :].rearrange("(i p) d -> p i d", p=128), xtok[:], reads=[t_xtok], writes=[to])
                    t_out.append(to)
                cl.append(store)
                return cl

            for cl in c_m1(0):
                cl()
            for i in range(NTT):
                ts_ = slice(i * 512, (i + 1) * 512)
                X, tX = XTs[i % 2], t_XTs[i % 2]
                for o in range(8):
                    res = {}
                    for nm, wsrc, wt, widx, rhs3, t_rhs, rot_ in (
                            ("yc", sq_s[0], t_sq[0], o, NTb, [t_NTb], PA), ("ga", wf_s, t_wf, 32 + o, U2T[:, :, ts_], [t_U2[i]], PB),
                            ("yg", sq_s[1], t_sq[1], o, OGt, [t_OGt], PA), ("gb", wf_s, t_wf, 40 + o, U2T[:, :, ts_], [t_U2[i]], PB)):
                        wb, t_wb = W2K.next()
                        dma("sp", wb[:], wsrc[widx], reads=[wt[widx]], writes=[t_wb])
                        ps, t_ps = rot_.next()
                        for kc in range(8):
                            mm(ps[:], wb[:, kc, :], rhs3[:, kc, :], kc == 0, kc == 7, [t_wb] + t_rhs, [t_ps])
                        res[nm] = (ps, t_ps)
                        drain(qB, 1)
                    sga, t_sga = tf.next()
                    act(sga[:], res["ga"][0][:], AF.Sigmoid, [res["ga"][1]], [t_sga])
                    y1, t_y1 = tf.next()
                    tt(y1[:], sga[:], res["yc"][0][:], ALU.mult, [t_sga, res["yc"][1]], [t_y1])
                    sgb, t_sgb = tf.next()
                    act(sgb[:], res["gb"][0][:], AF.Sigmoid, [res["gb"][1]], [t_sgb])
                    y2, t_y2 = tf.next()
                    tt(y2[:], sgb[:], res["yg"][0][:], ALU.mult, [t_sgb, res["yg"][1]], [t_y2])
                    tt(MTt[:, o, :], y1[:], y2[:], ALU.add, [t_y1, t_y2], [t_MTt])
                flush(qB)
                for o in range(8):
                    wb, t_wb = W2K.next()
                    dma("sp", wb[:], sq_s[2][o], reads=[t_sq[2][o]], writes=[t_wb])
                    pd, t_pd = PD.next()
                    for kc in range(8):
                        mm(pd[:], wb[:, kc, :], MTt[:, kc, :], kc == 0, kc == 7, [t_wb, t_MTt], [t_pd])
                    stt(X[:, o, :], pd[:], SC[:, colx, 5, o:o + 1], X[:, o, :], ALU.mult, ALU.add, [t_pd, t_SC, tX], [tX])
                    if o == 0:
                        qM = deque(stats_cl(X[:, :, :], [tX], 512, ones_bf[:]) +
                                   modulate_cl(uF[:, :, :], t_uF, X[:, :, :], [tX], 512, SC[:, colx, 6, :], SC[:, colx, 7, :]))
                    else:
                        drain(qM, 1)
                nxt = c_m1(i + 1) if i + 1 < NTT else []
                flush(qM)
                qA.extend(nxt)
                gate_up(512, 1, qA, 1)
                down(512, 1, SC[:, colx, 8, :], X, tX, qA, 3)
                flush(qA)
                qB.extend(c_m7(i))
            flush(qB)

        P.add("sp", lambda e: None, reads=t_out)
        P.add("pool", lambda e: None, reads=t_out)

        with nc.Block() as block:
            def run_eng(ename):
                def f(eng):
                    P.emit_one(ename, eng, sems, dsems)
                return f
            block.tensor(run_eng("pe"))
            block.scalar(run_eng("act"))
            block.vector(run_eng("dve"))
            block.gpsimd(run_eng("pool"))
            block.sync(run_eng("sp"))
    return nc, P


def _consts():
    a = np.arange(128)
    cst = np.zeros((128, 7, 128), np.float32)
    cst[:, 0, :] = np.eye(128, dtype=np.float32)
    cst[:, 1, :] = (a[:, None] <= a[None, :])
    cst[:, 2, :] = cst[:, 1, :]
    cst[:, 3, :] = (a[:, None] >= a[None, :])
    cst[:, 4, :] = cst[:, 3, :]
    cst[:, 5, :] = (a[:, None] > a[None, :])
    cst[:, 6, :] = (a[:, None] < a[None, :])
    return cst


def _pp(v):
    return np.ascontiguousarray(np.asarray(v, np.float32).reshape(-1, 128).T)


def make_in_maps(inp, nseq, ncores):
    f = lambda k: np.asarray(inp[k], np.float32)
    x, c, ctx = f("x"), f("c"), f("ctx")
    NT, NCX = x.shape[1], ctx.shape[1]
    vecs = np.zeros((128, 384), np.float32)
    for off, k in ((0, "g_ffn1"), (8, "g_mix"), (16, "g_ffn2"), (24, "g_final"), (32, "dw_bias"), (40, "conv_ln_g"),
                   (48, "conv_ln_b"), (56, "gla_norm_g")):
        vecs[:, off:off + 8] = _pp(f(k).reshape(-1))
    vecs[:, 64:136] = _pp(f("b_mod").reshape(-1))
    dw = f("dw_weight").reshape(31, D)
    vecs[:, 136:384] = dw.T.reshape(8, 128, 31).transpose(1, 0, 2).reshape(128, 248)
    wal = np.zeros((64, 512), np.float32)
    wal[0:16] = f("w_alpha_f").reshape(16, 512)
    wal[16] = f("b_alpha_f").reshape(512)
    wal[32:48] = f("w_alpha_b").reshape(16, 512)
    wal[48] = f("b_alpha_b").reshape(512)
    cst = _consts()
    shared = {
        "vecs": vecs, "walpha": wal, "consts": cst,
        "w_mod": f("w_mod")[0], "w1_gu": f("w1_gu")[0], "w1_down": f("w1_down")[0], "w_in": f("w_in")[0],
        "w_conv_out": f("w_conv_out")[0], "w_gla_out": f("w_gla_out")[0], "w_out": f("w_out")[0],
        "w2_gu": f("w2_gu")[0], "w2_down": f("w2_down")[0],
    }
    maps = []
    for i in range(ncores):
        sl = slice(i * nseq, (i + 1) * nseq)
        cv = np.concatenate([c[sl], f("c_ctx")[None, :]], axis=0)
        cT = np.ascontiguousarray(cv.T.reshape(8, 128, nseq + 1).transpose(1, 0, 2))
        m = dict(shared)
        m["x"] = np.ascontiguousarray(x[sl].reshape(nseq * NT, D))
        m["ctx"] = np.ascontiguousarray(ctx[sl].reshape(nseq * NCX, D))
        m["cT"] = cT
        maps.append(m)
    return maps


def kernel(**inputs):
    x = inputs["x"]
    B, NT, _ = x.shape
    NCX = inputs["ctx"].shape[1]
    nseq = B // NCORES
    nc, _ = build(nseq, NT, NCX)
    maps = make_in_maps(inputs, nseq, NCORES)
    res = run_bass_kernel_spmd(nc, maps, core_ids=list(range(NCORES)))
    out = np.concatenate([r["out"].reshape(nseq, NT, D) for r in res.results], axis=0)
    return out.astype(np.float32)
```

```python
import numpy as np
from contextlib import ExitStack
import concourse.bass as bass
import concourse.mybir as mybir
from concourse.bass_utils import run_bass_kernel_spmd

F32 = mybir.dt.float32
BF16 = mybir.dt.bfloat16
AF = mybir.ActivationFunctionType
ALU = mybir.AluOpType

ENGS = ("pe", "act", "dve", "pool", "sp")
NDMASEM = 8

D = 1024
DFF = 2816
NJ = DFF // 128
DIN = 7200
NMOD = 9
EPS = 1e-6
NCORES = 8


class T:
    __slots__ = ("name", "w", "r", "psum")

    def __init__(self, name, psum=False):
        self.name = name
        self.w = None
        self.r = []
        self.psum = psum


class Op:
    __slots__ = ("eng", "fn", "idx", "waits", "inc", "dma", "sem", "semval", "vc", "cnt")

    def __init__(self, eng, fn, idx, dma):
        self.eng = eng
        self.fn = fn
        self.idx = idx
        self.dma = dma
        self.waits = []
        self.inc = False
        self.sem = None
        self.semval = 0
        self.vc = None
        self.cnt = 0


class Prog:
    def __init__(self):
        self.ops = {e: [] for e in ENGS}
        self.clock = {e: {} for e in ENGS}
        self.ndma = {e: 0 for e in ENGS}
        self.dma_ops = {e: [] for e in ENGS}
        self._prepared = False

    def add(self, eng, fn, reads=(), writes=(), dma=False):
        op = Op(eng, fn, len(self.ops[eng]), dma)
        clk = self.clock[eng]
        deps = []

        def need(o, kind):
            if o is None:
                return
            if o.eng == eng and not o.dma and not dma and eng != "pool":
                if eng == "pe" or kind == "waw":
                    return
            deps.append(o)

        for t in reads:
            need(t.w, "raw")
            if t.psum:
                for o in t.r:
                    if o.eng != eng:
                        need(o, "war")
        for t in writes:
            need(t.w, "waw")
            for o in t.r:
                need(o, "war")
        if dma:
            k = self.ndma[eng]
            self.ndma[eng] = k + 1
            op.sem = k % NDMASEM
            op.semval = 16 * (k // NDMASEM + 1)
            if k >= NDMASEM:
                deps.append(self.dma_ops[eng][k - NDMASEM])
            self.dma_ops[eng].append(op)
        best = {}
        for o in deps:
            key = (o.eng, "d", o.sem) if o.dma else (o.eng, "c")
            val = o.semval if o.dma else o.idx
            if key not in best or best[key][0] < val:
                best[key] = (val, o)
        for key, (val, o) in best.items():
            if clk.get(key, -1) >= val:
                continue
            op.waits.append(o)
            o.inc = True
            for k2, v2 in o.vc.items():
                if clk.get(k2, -1) < v2:
                    clk[k2] = v2
            clk[key] = val
        op.vc = dict(clk)
        for t in writes:
            t.w = op
            t.r = []
        for t in reads:
            t.r.append(op)
        self.ops[eng].append(op)
        return op

    def prepare(self):
        for e in ENGS:
            c = 0
            for o in self.ops[e]:
                if not o.dma and o.inc:
                    c += 1
                    o.cnt = c

    def emit_one(self, e, eng, sems, dsems):
        if not self._prepared:
            self.prepare()
            self._prepared = True
        for o in self.ops[e]:
            best = {}
            for d in o.waits:
                if d.dma:
                    key = ("d", d.eng, d.sem)
                    v = d.semval
                else:
                    key = ("c", d.eng)
                    v = d.cnt
                if best.get(key, -1) < v:
                    best[key] = v
            for key, v in best.items():
                if key[0] == "d":
                    eng.wait_ge(dsems[key[1]][key[2]], v)
                else:
                    eng.wait_ge(sems[key[1]], v)
            ins = o.fn(eng)
            if ins is None:
                continue
            if o.dma:
                ins.then_inc(dsems[e][o.sem], 16)
            elif o.inc:
                ins.then_inc(sems[e], 1)


class Rot:
    def __init__(self, items):
        self.items = items
        self.i = 0

    def next(self):
        it = self.items[self.i % len(self.items)]
        self.i += 1
        return it


def build(nseq, NT, NCX):
    assert NT % 512 == 0 and NCX % 128 == 0 and NCX <= 512
    NTT = NT // 512
    NXB = NT // 128
    NCB = NCX // 128
    TOK = NT + NCX
    NB = TOK // 128
    NCOL = nseq + 1
    nc = bass.Bass("TRN2", target_bir_lowering=False)

    def dram_in(name, shape, dt=F32):
        return nc.dram_tensor(name, shape, dt, kind="ExternalInput").ap()

    def dram_sc(name, shape, dt):
        return nc.dram_tensor(name, shape, dt, kind="Internal").ap()

    x_d = dram_in("x", [nseq * NT, D])
    ctx_d = dram_in("ctx", [nseq * NCX, D])
    cT_d = dram_in("cT", [128, 8, NCOL])
    vecs_d = dram_in("vecs", [128, 384])
    wal_d = dram_in("walpha", [64, 512])
    cst_d = dram_in("consts", [128, 7, 128])
    wmod_d = dram_in("w_mod", [D, NMOD * D])
    w1gu_d = dram_in("w1_gu", [D, 2 * DFF])
    w1d_d = dram_in("w1_down", [DFF, D])
    win_d = dram_in("w_in", [D, DIN])
    wco_d = dram_in("w_conv_out", [D, D])
    wgo_d = dram_in("w_gla_out", [D, D])
    wout_d = dram_in("w_out", [D, D])
    w2gu_d = dram_in("w2_gu", [D, 2 * DFF])
    w2d_d = dram_in("w2_down", [DFF, D])
    out_d = nc.dram_tensor("out", [nseq * NT, D], F32, kind="ExternalOutput").ap()

    gu_s = [dram_sc(f"gu_s{i}", [NJ, 128, 8, 256], BF16) for i in range(2)]
    dn_s = [dram_sc(f"dn_s{i}", [8, 128, NJ, 128], BF16) for i in range(2)]
    wf_s = dram_sc("wf_s", [48, 128, 8, 128], BF16)
    wkv_s = dram_sc("wkv_s", [2, 128, 8, 768], BF16)
    wa_s = dram_sc("wa_s", [128, 8, 32], BF16)
    sq_s = [dram_sc(f"sq_s{i}", [8, 128, 8, 128], BF16) for i in range(3)]
    x1_s = dram_sc("x1_s", [nseq, D, NT], F32)
    og_s = dram_sc("og_s", [nseq, D, NT], BF16)
    cv_s = dram_sc("cv_s", [nseq, D, NT], BF16)

    P = Prog()
    es = ExitStack()
    with es:
        cur = [16512]
        SB_END = 229376

        def nbytes(shape, dt):
            n = 1
            for v in shape[1:]:
                n *= v
            return ((n * (4 if dt == F32 else 2) + 63) // 64) * 64

        def sb(name, shape, dt, at=None):
            nb = nbytes(shape, dt)
            if at is None:
                off = cur[0]
                cur[0] += nb
                assert cur[0] <= SB_END, f"SBUF overflow at {name}: {cur[0]}"
            else:
                off = at[0]
                at[0] += nb
            return nc.alloc_sbuf_tensor_at(name, shape, dt, offset=off)

        def psum(name):
            return es.enter_context(nc.psum_tensor(name, [128, 512], F32))

        def rot(name, n, shape, dt, at=None):
            return Rot([(sb(f"{name}{i}", shape, dt, at), T(f"{name}{i}")) for i in range(n)])

        def rot_ts(r):
            return [t for (_, t) in r.items]

        cst = sb("cst", [128, 7, 128], F32); t_cst = T("cst")
        ident = cst[:, 0, :]
        M_le, M_ge, M_gt, M_lt = cst[:, 1, :], cst[:, 3, :], cst[:, 5, :], cst[:, 6, :]
        M_le2, M_ge2 = cst[:, 1:3, :], cst[:, 3:5, :]
        ident_bf = sb("ident_bf", [128, 128], BF16); t_identbf = T("identbf")
        ones_d = sb("ones_d", [128, 128], F32)
        ones_h = sb("ones_h", [128, 128], BF16)
        ones_bf = sb("ones_bf", [128, 128], BF16)
        t_ones = T("ones")
        kc_ = sb("kconst", [128, 2], F32); t_kc = T("kconst")
        vecs = sb("vecs", [128, 384], F32); t_vecs = T("vecs")
        wal = sb("wal", [64, 512], F32); t_wal = T("wal")
        modT = sb("modT", [128, 72, NCOL], F32); t_modT = T("modT")
        SC = sb("SC", [128, NCOL, 9, 8], F32); t_SC = T("SC")
        scT = sb("scT", [128, 8, NCOL], F32); t_scT = T("scT")
        V_GF1, V_GMIX, V_GF2, V_GFIN, V_DWB, V_LNG, V_LNB, V_GN, V_BMOD, V_DW = 0, 8, 16, 24, 32, 40, 48, 56, 64, 136

        pbank = [psum(f"pb{i}") for i in range(8)]
        t_pb = [T(f"pb{i}", psum=True) for i in range(8)]
        PA = Rot([(pbank[0], t_pb[0]), (pbank[1], t_pb[1])])
        PB = Rot([(pbank[2], t_pb[2]), (pbank[3], t_pb[3])])
        PD = Rot([(pbank[4], t_pb[4]), (pbank[5], t_pb[5])])
        PS_ = (pbank[6], t_pb[6])
        PT = (pbank[7], t_pb[7])

        U2T = sb("U2T", [128, 8, TOK], BF16)
        t_U2 = [T(f"U2_{i}") for i in range(NTT + 1)]
        def u2_tile_of_block(b):
            return t_U2[b // 4] if b < NXB else t_U2[NTT]

        tf = rot("tf", 4, [128, 512], F32)
        sqr = rot("sqr", 3, [128, 512], BF16)
        tb = rot("tb", 2, [128, 512], BF16)
        rstd_ = sb("rstd", [128, 512], F32); t_rstd = T("rstd")
        mean_ = sb("mean", [128, 512], F32); t_mean = T("mean")
        W2K = rot("w2k", 4, [128, 8, 128], BF16)
        W4K = rot("w4k", 3, [128, 8, 256], BF16)
        WDN = rot("wdn", 2, [128, NJ, 128], BF16)
        WA = sb("wa", [128, 8, 32], BF16); t_WA = T("wa")

        arena0 = cur[0]
        p = [arena0]
        xtok = sb("xtok", [128, 4, D], F32, p); t_xtok = T("xtok")
        XTs = [sb(f"XT{k}", [128, 8, 512], F32, p) for k in range(2)]; t_XTs = [T(f"XT{k}") for k in range(2)]
        uF = sb("uF", [128, 8, 512], BF16, p); t_uF = [T(f"uF{c}") for c in range(8)]
        hF = sb("hF", [128, NJ, 512], BF16, p); t_hF = [T(f"hF{j}") for j in range(NJ)]
        CVt = sb("CVt", [128, 8, 512], BF16, p); t_CVt = T("CVt")
        OGt = sb("OGt", [128, 8, 512], BF16, p); t_OGt = T("OGt")
        NTb = sb("NTb", [128, 8, 512], BF16, p); t_NTb = T("NTb")
        MTt = sb("MTt", [128, 8, 512], BF16, p); t_MTt = T("MTt")
        end_ffn = p[0]
        SET_FFN = [t_xtok] + t_XTs + t_uF + t_hF + [t_CVt, t_OGt, t_NTb, t_MTt]
        p = [arena0]
        lrT = sb("lrT", [64, TOK], F32, p); t_lrT = T("lrT")
        WKV = rot("wkv", 1, [128, 8, 768], BF16, p)
        qT2 = sb("qT2", [128, 2, NT], BF16, p); t_qT = T("qT")
        kT2 = sb("kT2", [128, 2, TOK], BF16, p); t_kT = T("kT")
        kt2 = sb("kt2", [128, NB, 256], BF16, p); t_kt = T("kt")
        vt2 = sb("vt2", [128, NB, 512], BF16, p); t_vt = T("vt")
        OT2 = sb("OT2", [128, 4, NT], BF16, p); t_OT = [T(f"OT{i}") for i in range(NXB)]
        S32 = [sb(f"S32_{d}", [128, 2, 256], F32, p) for d in range(2)]; t_S32 = [T(f"S32_{d}") for d in range(2)]
        Sbf = [rot(f"Sbf{d}_", 2, [128, 2, 256], BF16, p) for d in range(2)]
        g32 = rot("g32", 2, [128, 256], F32, p)
        ge = rot("ge", 2, [128, 256], F32, p)
        e1 = rot("e1", 4, [128, 2, 128], F32, p)
        e2 = rot("e2", 2, [128, 2, 128], F32, p)
        e3 = rot("e3", 2, [128, 256], F32, p)
        qd = rot("qd", 4, [128, 2, 128], BF16, p)
        kd = rot("kd", 2, [128, 2, 128], BF16, p)
        kh = rot("kh", 2, [128, 256], BF16, p)
        am = rot("am", 2, [128, 2, 128], BF16, p)
        end_gla = p[0]
        SET_GLA = [t_lrT, t_qT, t_kT, t_kt, t_vt] + t_OT + t_S32 + rot_ts(WKV)
        for r_ in (Sbf[0], Sbf[1], g32, ge, e1, e2, e3, qd, kd, kh, am):
            SET_GLA += rot_ts(r_)
        p = [arena0]
        ZT = rot("ZT", 2, [128, NT + 30], BF16, p)
        DG = rot("DG", 2, [128, 31, 128], BF16, p)
        CVc = rot("CVc", 2, [128, NT], BF16, p)
        end_conv = p[0]
        SET_CONV = rot_ts(ZT) + rot_ts(DG) + rot_ts(CVc)
        cur[0] = max(end_ffn, end_gla, end_conv)
        assert cur[0] <= SB_END, f"SBUF overflow: {cur[0]}"
        build.sbuf_report = dict(arena0=arena0, end_ffn=end_ffn, end_gla=end_gla, end_conv=end_conv, top=cur[0], limit=SB_END)

        def phase_alias(new_ts, old_ts):
            ops_ = []
            seen = set()
            for t in old_ts:
                for o in ([t.w] if t.w is not None else []) + t.r:
                    if id(o) not in seen:
                        seen.add(id(o))
                        ops_.append(o)
            for t in new_ts:
                t.r.extend(ops_)

        sems = {e: es.enter_context(nc.semaphore(f"s_{e}")) for e in ENGS}
        dsems = {e: [es.enter_context(nc.semaphore(f"d_{e}{i}")) for i in range(NDMASEM)] for e in ENGS}

        def dma(q, out, in_, reads=(), writes=(), **kw):
            return P.add(q, lambda e: e.dma_start(out=out, in_=in_, **kw), reads=reads, writes=writes, dma=True)

        def mm(out, lhsT, rhs, start, stop, reads, writes):
            return P.add("pe", lambda e: e.matmul(out, lhsT, rhs, start=start, stop=stop), reads=reads, writes=writes)

        def act(out, in_, func, reads, writes, bias=None, scale=None):
            kw = {}
            if bias is not None:
                kw["bias"] = bias
            if scale is not None:
                kw["scale"] = scale
            return P.add("act", lambda e: e.activation(out, in_, func, **kw), reads=reads, writes=writes)

        def tt(out, in0, in1, op, reads, writes, eng="dve"):
            return P.add(eng, lambda e: e.tensor_tensor(out, in0, in1, op), reads=reads, writes=writes)

        def stt(out, in0, scalar, in1, op0, op1, reads, writes):
            return P.add("dve", lambda e: e.scalar_tensor_tensor(out=out, in0=in0, scalar=scalar, in1=in1, op0=op0, op1=op1),
                         reads=reads, writes=writes)

        dma("sp", cst[:], cst_d, writes=[t_cst])
        dma("sp", vecs[:], vecs_d, writes=[t_vecs])
        dma("sp", wal[:], wal_d, writes=[t_wal])
        dma("sp", scT[:], cT_d, writes=[t_scT])
        P.add("dve", lambda e: e.memset(ones_d[:], 1.0 / D), writes=[t_ones])
        P.add("dve", lambda e: e.memset(ones_h[:], 1.0 / 256), writes=[t_ones])
        P.add("dve", lambda e: e.memset(ones_bf[:], 1.0 / D), writes=[t_ones])
        P.add("dve", lambda e: e.memset(kc_[:, 0:1], EPS), writes=[t_kc])
        P.add("dve", lambda e: e.memset(kc_[:, 1:2], 1.0), writes=[t_kc])
        P.add("dve", lambda e: e.tensor_copy(ident_bf[:], ident), reads=[t_cst], writes=[t_identbf])
        eps_ap = kc_[:, 0:1]
        one_ap = kc_[:, 1:2]

        t_gu = [[T(f"gu{i}_{j}") for j in range(NJ)] for i in range(2)]
        t_dn = [[T(f"dn{i}_{o}") for o in range(8)] for i in range(2)]
        t_wf = [T(f"wf{i}") for i in range(48)]
        t_wkv = [T(f"wkv{h}") for h in range(2)]
        t_was = T("was")
        t_sq = [[T(f"sq{i}_{o}") for o in range(8)] for i in range(3)]

        def cast_gu2(i, src):
            for j in range(NJ):
                ta = T("tmp")
                dma("pool", gu_s[i][j, :, :, 0:128], src[:, j * 128:(j + 1) * 128].rearrange("(kc p) c -> p kc c", p=128), writes=[ta])
                dma("pool", gu_s[i][j, :, :, 128:256], src[:, DFF + j * 128:DFF + (j + 1) * 128].rearrange("(kc p) c -> p kc c", p=128),
                    reads=[ta], writes=[t_gu[i][j]])

        def cast_dn(i, src):
            for o in range(8):
                dma("pool", dn_s[i][o], src[:, o * 128:(o + 1) * 128].rearrange("(j p) c -> p j c", p=128), writes=[t_dn[i][o]])

        def cast_sq(i, src):
            for o in range(8):
                dma("pool", sq_s[i][o], src[:, o * 128:(o + 1) * 128].rearrange("(kc p) c -> p kc c", p=128), writes=[t_sq[i][o]])

        def cast_wf(idx, col0):
            dma("pool", wf_s[idx], win_d[:, col0:col0 + 128].rearrange("(kc p) c -> p kc c", p=128), writes=[t_wf[idx]])

        cast_gu2(0, w1gu_d)
        cast_dn(0, w1d_d)
        dma("pool", wa_s, win_d[:, 5120:5152].rearrange("(kc p) c -> p kc c", p=128), writes=[t_was])
        for h in range(4):
            cast_wf(16 + h, 2048 + h * 128)
            cast_wf(20 + h, 2560 + h * 128)
            if h % 2 == 0:
                hp_ = h // 2
                ta = T("tmp")
                dma("pool", wkv_s[hp_, :, :, 0:256], win_d[:, 2560 + hp_ * 256:2560 + (hp_ + 1) * 256].rearrange("(kc p) c -> p kc c", p=128), writes=[ta])
                dma("pool", wkv_s[hp_, :, :, 256:768], win_d[:, 3072 + hp_ * 512:3072 + (hp_ + 1) * 512].rearrange("(kc p) c -> p kc c", p=128),
                    reads=[ta], writes=[t_wkv[hp_]])
            cast_wf(24 + 2 * h, 4096 + (2 * h) * 128)
            cast_wf(24 + 2 * h + 1, 4096 + (2 * h + 1) * 128)
        def late_casts():
            for c in range(8):
                cast_wf(c, c * 128)
                cast_wf(8 + c, 1024 + c * 128)
            cast_sq(0, wco_d)
            cast_sq(1, wgo_d)
            for c in range(8):
                cast_wf(32 + c, 5152 + c * 128)
                cast_wf(40 + c, 6176 + c * 128)
            cast_sq(2, wout_d)
            cast_gu2(1, w2gu_d)
            cast_dn(1, w2d_d)

        P.add("act", lambda e: e.activation(scT[:], scT[:], AF.Silu), reads=[t_scT], writes=[t_scT])
        wm_bufs = [(xtok[:].rearrange("p a (b c) -> p (a b) c", c=512), t_xtok), (XTs[0][:], t_XTs[0]), (XTs[1][:], t_XTs[1])]
        pmod, t_pmod = PS_
        for ch in range(18):
            wm_view, t_wm = wm_bufs[ch % 3]
            dma("sp", wm_view, wmod_d[:, ch * 512:(ch + 1) * 512].rearrange("(kc p) c -> p kc c", p=128), writes=[t_wm])
            pr, t_pr = PA.next()
            for kc in range(8):
                mm(pr[0:NCOL, :], scT[:, kc, :], wm_view[:, kc, :], kc == 0, kc == 7, [t_wm, t_scT], [t_pr])
            rowt, t_rowt = tf.next()
            act(rowt[0:NCOL, :], pr[0:NCOL, :], AF.Copy, [t_pr], [t_rowt])
            for sub in range(4):
                j = ch * 4 + sub
                P.add("pe", lambda e, j=j, sub=sub, rowt=rowt: e.transpose(pmod[:, j * NCOL:(j + 1) * NCOL], rowt[0:NCOL, sub * 128:(sub + 1) * 128],
                                                                         ident[0:NCOL, 0:NCOL]), reads=[t_rowt, t_cst], writes=[t_pmod])
        for col in range(NCOL):
            P.add("dve", lambda e, col=col: e.tensor_tensor(
                modT[:, :, col], pmod[:, 0:72 * NCOL].rearrange("p (j c) -> p j c", c=NCOL)[:, :, col],
                vecs[:, V_BMOD:V_BMOD + 72], ALU.add), reads=[t_pmod, t_vecs], writes=[t_modT])
        for col in range(NCOL):
            for n, gv in enumerate((V_GF1, V_GMIX, V_GF2)):
                sh = modT[:, (3 * n) * 8:(3 * n) * 8 + 8, col]
                scl = modT[:, (3 * n + 1) * 8:(3 * n + 1) * 8 + 8, col]
                gt = modT[:, (3 * n + 2) * 8:(3 * n + 2) * 8 + 8, col]
                P.add("dve", lambda e, col=col, n=n, scl=scl, gv=gv: e.scalar_tensor_tensor(
                    out=SC[:, col, 3 * n, :], in0=scl, scalar=1.0, in1=vecs[:, gv:gv + 8], op0=ALU.add, op1=ALU.mult),
                    reads=[t_modT, t_vecs], writes=[t_SC])
                P.add("dve", lambda e, col=col, n=n, sh=sh: e.tensor_copy(SC[:, col, 3 * n + 1, :], sh), reads=[t_modT], writes=[t_SC])
                P.add("dve", lambda e, col=col, n=n, gt=gt: e.tensor_scalar(
                    SC[:, col, 3 * n + 2, :], gt, (1.0 if n == 1 else 0.5), None, ALU.mult), reads=[t_modT], writes=[t_SC])

        from collections import deque
        qA = deque()
        qB = deque()

        def drain(q, k):
            for _ in range(k):
                if q:
                    q.popleft()()

        def flush(q):
            while q:
                q.popleft()()

        def stats_cl(src3, t_src, W, ones_ap, nchunk=8):
            pst, t_pst = PS_
            sqs = {}

            def sqf(c):
                sq, t_sqq = sqr.next()
                act(sq[:, :W], src3[:, c, :], AF.Square, t_src, [t_sqq])
                sqs[c] = (sq, t_sqq)

            def mk(c):
                def f():
                    if c == 0:
                        sqf(0)
                    if c + 1 < nchunk:
                        sqf(c + 1)
                    sq, t_sqq = sqs.pop(c)
                    mm(pst[:, :W], ones_ap, sq[:, :W], c == 0, c == nchunk - 1, [t_ones, t_sqq], [t_pst])
                return f

            def fin():
                tmp, t_tmp = tf.next()
                act(tmp[:, :W], pst[:, :W], AF.Sqrt, [t_pst, t_kc], [t_tmp], bias=eps_ap)
                P.add("dve", lambda e: e.reciprocal(rstd_[:, :W], tmp[:, :W]), reads=[t_tmp], writes=[t_rstd])
            return [mk(c) for c in range(nchunk)] + [fin]

        def modulate_cl(dst3, t_dst, src3, t_src, W, A, Bv):
            def mk(c):
                def f():
                    tmp, t_tmp = tf.next()
                    stt(tmp[:, :W], src3[:, c, :], A[:, c:c + 1], rstd_[:, :W], ALU.mult, ALU.mult, t_src + [t_SC, t_rstd], [t_tmp])
                    act(dst3[:, c, :], tmp[:, :W], AF.Identity, [t_tmp, t_SC], [t_dst[c] if isinstance(t_dst, list) else t_dst],
                        bias=Bv[:, c:c + 1])
                return f
            return [mk(c) for c in range(8)]

        def gate_up(W, wi, q=None, per=1):
            for j in range(NJ):
                wb, t_wb = W4K.next()
                dma("sp", wb[:], gu_s[wi][j], reads=[t_gu[wi][j]], writes=[t_wb])
                pa, t_pa = PA.next()
                pb, t_pbb = PB.next()
                for kc in range(8):
                    mm(pa[:, :W], wb[:, kc, 0:128], uF[:, kc, :W], kc == 0, kc == 7, [t_wb, t_uF[kc]], [t_pa])
                if q is not None:
                    drain(q, per)
                for kc in range(8):
                    mm(pb[:, :W], wb[:, kc, 128:256], uF[:, kc, :W], kc == 0, kc == 7, [t_wb, t_uF[kc]], [t_pbb])
                if q is not None:
                    drain(q, per)
                sa, t_sa = tf.next()
                act(sa[:, :W], pa[:, :W], AF.Silu, [t_pa], [t_sa])
                tt(hF[:, j, :W], sa[:, :W], pb[:, :W], ALU.mult, [t_sa, t_pbb], [t_hF[j]])

        def down(W, wi, G, X, t_X, q=None, per=3):
            for o in range(8):
                wb, t_wb = WDN.next()
                dma("sp", wb[:], dn_s[wi][o], reads=[t_dn[wi][o]], writes=[t_wb])
                pd, t_pd = PD.next()
                for j in range(NJ):
                    mm(pd[:, :W], wb[:, j, :], hF[:, j, :W], j == 0, j == NJ - 1, [t_wb, t_hF[j]], [t_pd])
                stt(X[:, o, :W], pd[:, :W], G[:, o:o + 1], X[:, o, :W], ALU.mult, ALU.add, [t_pd, t_SC, t_X], [t_X])
                if q is not None:
                    drain(q, per)

        def load_tok(src_rows, W):
            nsub = W // 128
            dma("sp", xtok[:, 0:nsub, :], src_rows.rearrange("(i p) d -> p i d", p=128), writes=[t_xtok])

        def transpose_in(W, X, t_X):
            nsub = W // 128
            for c in range(8):
                ps, t_ps = PA.next()
                for i in range(nsub):
                    P.add("pe", lambda e, ps=ps, i=i, c=c: e.transpose(ps[:, i * 128:(i + 1) * 128], xtok[:, i, c * 128:(c + 1) * 128], ident),
                          reads=[t_xtok, t_cst], writes=[t_ps])
                act(X[:, c, :W], ps[:, :W], AF.Copy, [t_ps], [t_X])

        t_out = []
        for s in range(nseq):
            colx, colc = s, nseq
            t_x1 = [T(f"x1_{s}_{i}") for i in range(NTT)]
            t_ogs = [T(f"ogs_{s}_{h}") for h in range(2)]
            t_cvs = [T(f"cvs_{s}_{c}") for c in range(8)]

            def a_info(ti):
                isx = ti < NTT
                W = 512 if isx else NCX
                col = colx if isx else colc
                tok0 = ti * 512 if isx else NT
                src = x_d[s * NT + ti * 512: s * NT + ti * 512 + 512, :] if isx else ctx_d[s * NCX:(s + 1) * NCX, :]
                return isx, W, col, tok0, src

            def a_n1(ti):
                isx, W, col, tok0, src = a_info(ti)
                X, tX = XTs[ti % 2], t_XTs[ti % 2]
                return (stats_cl(X[:, :, :W], [tX], W, ones_bf[:]) +
                        modulate_cl(uF[:, :, :W], t_uF, X[:, :, :W], [tX], W, SC[:, col, 0, :], SC[:, col, 1, :]))

            def a_n2(ti):
                isx, W, col, tok0, src = a_info(ti)
                X, tX = XTs[ti % 2], t_XTs[ti % 2]
                return (stats_cl(X[:, :, :W], [tX], W, ones_bf[:]) +
                        modulate_cl(U2T[:, :, tok0:tok0 + W], t_U2[ti], X[:, :, :W], [tX], W, SC[:, col, 3, :], SC[:, col, 4, :]))

            isx, W, col, tok0, src = a_info(0)
            load_tok(src, W)
            transpose_in(W, XTs[0], t_XTs[0])
            for cl in a_n1(0):
                cl()
            for ti in range(NTT + 1):
                isx, W, col, tok0, src = a_info(ti)
                X, tX = XTs[ti % 2], t_XTs[ti % 2]
                if ti + 1 <= NTT:
                    isx1, W1, col1, tok1, src1 = a_info(ti + 1)
                    load_tok(src1, W1)
                gate_up(W, 0, qB, 1)
                flush(qB)
                if ti + 1 <= NTT:
                    transpose_in(W1, XTs[(ti + 1) % 2], t_XTs[(ti + 1) % 2])
                    qA.extend(a_n1(ti + 1))
                down(W, 0, SC[:, col, 2, :], X, tX, qA, 3)
                flush(qA)
                if isx:
                    dma("act", x1_s[s, :, ti * 512:(ti + 1) * 512].rearrange("(c p) t -> p c t", p=128), X[:], reads=[tX], writes=[t_x1[ti]])
                qB.extend(a_n2(ti))
            flush(qB)

            phase_alias(SET_GLA, SET_FFN)
            if s == 0:
                late_casts()
            P.add("dve", lambda e: e.memset(lrT[:], 1.0), writes=[t_lrT])
            dma("sp", WA[:], wa_s, reads=[t_was], writes=[t_WA])
            tiles = [(i * 512, 512, t_U2[i]) for i in range(NTT)] + [(NT, NCX, t_U2[NTT])]
            for (t0, W, tu) in tiles:
                ps, t_ps = PA.next()
                for half in range(2):
                    for kc in range(8):
                        mm(ps[32 * half:32 * half + 16, :W], WA[:, kc, 16 * half:16 * half + 16], U2T[:, kc, t0:t0 + W], kc == 0, kc == 7,
                           [t_WA, tu], [t_ps])
                for half in range(2):
                    act(lrT[32 * half:32 * half + 16, t0:t0 + W], ps[32 * half:32 * half + 16, :W], AF.Copy, [t_ps], [t_lrT])

            for hp in range(2):
                for hh in range(2):
                    h = 2 * hp + hh
                    wq, t_wq = W2K.next()
                    dma("sp", wq[:], wf_s[16 + h], reads=[t_wf[16 + h]], writes=[t_wq])
                    for i in range(NTT):
                        ps, t_ps = PA.next()
                        for kc in range(8):
                            mm(ps[:, :], wq[:, kc, :], U2T[:, kc, i * 512:(i + 1) * 512], kc == 0, kc == 7, [t_wq, t_U2[i]], [t_ps])
                        P.add("act", lambda e, i=i, ps=ps, hh=hh: e.mul(qT2[:, hh, i * 512:(i + 1) * 512], ps[:, :], float(128 ** -0.5)),
                              reads=[t_ps], writes=[t_qT])
                    wk, t_wk = W2K.next()
                    dma("sp", wk[:], wf_s[20 + h], reads=[t_wf[20 + h]], writes=[t_wk])
                    for (t0, W, tu) in tiles:
                        ps, t_ps = PB.next()
                        for kc in range(8):
                            mm(ps[:, :W], wk[:, kc, :], U2T[:, kc, t0:t0 + W], kc == 0, kc == 7, [t_wk, tu], [t_ps])
                        P.add("dve", lambda e, ps=ps, t0=t0, W=W, hh=hh: e.tensor_copy(kT2[:, hh, t0:t0 + W], ps[:, :W]), reads=[t_ps], writes=[t_kT])
                wv, t_wv = WKV.next()
                dma("sp", wv[:], wkv_s[hp], reads=[t_wkv[hp]], writes=[t_wv])
                for b in range(NB):
                    pk, t_pk = PA.next()
                    pv, t_pv = PD.next()
                    tu = u2_tile_of_block(b)
                    for kc in range(8):
                        mm(pk[:, 0:256], U2T[:, kc, b * 128:(b + 1) * 128], wv[:, kc, 0:256], kc == 0, kc == 7, [t_wv, tu], [t_pk])
                    for kc in range(8):
                        mm(pv[:, :], U2T[:, kc, b * 128:(b + 1) * 128], wv[:, kc, 256:768], kc == 0, kc == 7, [t_wv, tu], [t_pv])
                    P.add("dve", lambda e, pk=pk, b=b: e.tensor_copy(kt2[:, b, :], pk[:, 0:256]), reads=[t_pk], writes=[t_kt])
                    act(vt2[:, b, :], pv[:, :], AF.Copy, [t_pv], [t_vt])

                ctxb = list(range(NXB, NB))
                xb = list(range(NXB))
                ordf = ctxb + xb
                ordb = ctxb[::-1] + xb[::-1]
                slots = []
                for k in range(NB):
                    slots.append((0, ordf[k]))
                    slots.append((1, ordb[k]))
                nsl = len(slots)
                sbcur = []
                for d in range(2):
                    P.add("dve", lambda e, d=d: e.memset(S32[d][:], 0.0), writes=[t_S32[d]])
                    sb0, t_sb0 = Sbf[d].next()
                    P.add("dve", lambda e, sb0=sb0: e.memset(sb0[:], 0.0), writes=[t_sb0])
                    sbcur.append((sb0, t_sb0))
                visited = set()
                ST = [dict() for _ in range(nsl)]
                ZR = [(pbank[0], t_pb[0]), (pbank[1], t_pb[1])]
                BA = [(pbank[2], t_pb[2]), (pbank[3], t_pb[3])]
                SN = [(pbank[4], t_pb[4]), (pbank[5], t_pb[5])]
                OB = [(pbank[6], t_pb[6]), (pbank[7], t_pb[7])]

                def gstage(j, i, hp=hp):
                    d, b = slots[i]
                    st = ST[i]
                    base = 32 * d
                    Mcum = M_le if d == 0 else M_ge
                    Mrem = M_gt if d == 0 else M_lt
                    Mmask2 = M_le2 if d == 0 else M_ge2
                    last = 127 if d == 0 else 0
                    isx = b < NXB
                    tok = slice(b * 128, (b + 1) * 128)
                    zr, t_zr = ZR[i % 2]
                    ba, t_ba = BA[i % 2]
                    sn, t_sn = SN[i % 2]
                    ob, t_ob = OB[i % 2]
                    if j == 0:
                        mm(zr[:, 0:256], lrT[base:base + 17, tok], wal[base:base + 17, hp * 256:(hp + 1) * 256], True, True,
                           [t_lrT, t_wal], [t_zr])
                    elif j == 1:
                        gE, t_gE = ge.next()
                        act(gE[:], zr[:, 0:256], AF.Exp, [t_zr], [t_gE], scale=-1.0)
                        gS, t_gS = g32.next()
                        act(gS[:], gE[:], AF.Ln, [t_gE, t_kc], [t_gS], bias=one_ap)
                        st.update(gS=gS, t_gS=t_gS)
                    elif j == 2:
                        gS, t_gS = st["gS"], st["t_gS"]
                        mm(zr[:, 256:512], Mrem, gS[:], True, True, [t_gS, t_cst], [t_zr])
                        for hh in range(2):
                            mm(ba[:, hh * 128:(hh + 1) * 128], gS[:, hh * 128:(hh + 1) * 128], Mcum, True, True, [t_gS, t_cst], [t_ba])
                    elif j == 3:
                        E1, t_E1 = e1.next()
                        act(E1[:], ba[:, 0:256].rearrange("p (h t) -> p h t", h=2), AF.Exp, [t_ba], [t_E1], scale=-1.0 / 16)
                        E3, t_E3 = e3.next()
                        act(E3[:], zr[:, 256:512], AF.Exp, [t_zr], [t_E3], scale=-1.0 / 16)
                        st.update(E1=E1, t_E1=t_E1, E3=E3, t_E3=t_E3)
                        if isx:
                            E2, t_E2 = e2.next()
                            act(E2[:], ba[:, 0:256].rearrange("p (h t) -> p h t", h=2), AF.Exp, [t_ba], [t_E2], scale=1.0 / 16)
                            st.update(E2=E2, t_E2=t_E2)
                    elif j == 4:
                        E1, t_E1, E3, t_E3 = st["E1"], st["t_E1"], st["E3"], st["t_E3"]
                        KH, t_KH = kh.next()
                        tt(KH[:], kt2[:, b, :], E3[:], ALU.mult, [t_kt, t_E3], [t_KH])
                        st.update(KH=KH, t_KH=t_KH)
                        if isx:
                            E2, t_E2 = st["E2"], st["t_E2"]
                            QD, t_QD = qd.next()
                            qk_eng = "pool" if s > 0 else "dve"
                            tt(QD[:], qT2[:, :, tok], E1[:], ALU.mult, [t_qT, t_E1], [t_QD], eng=qk_eng)
                            KD, t_KD = kd.next()
                            tt(KD[:], kT2[:, :, tok], E2[:], ALU.mult, [t_kT, t_E2], [t_KD], eng=qk_eng)
                            st.update(QD=QD, t_QD=t_QD, KD=KD, t_KD=t_KD)
                    elif j == 5:
                        KH, t_KH = st["KH"], st["t_KH"]
                        for hh in range(2):
                            mm(sn[:, hh * 256:(hh + 1) * 256], KH[:, hh * 128:(hh + 1) * 128], vt2[:, b, hh * 256:(hh + 1) * 256], True, True,
                               [t_KH, t_vt], [t_sn])
                        if isx:
                            QD, t_QD, KD, t_KD = st["QD"], st["t_QD"], st["KD"], st["t_KD"]
                            for hh in range(2):
                                mm(ba[:, 256 + hh * 128:256 + (hh + 1) * 128], KD[:, hh, :], QD[:, hh, :], True, True, [t_KD, t_QD], [t_ba])
                    elif j == 6:
                        E1, t_E1 = st["E1"], st["t_E1"]
                        for hh in range(2):
                            stt(S32[d][:, hh, :], S32[d][:, hh, :], E1[:, hh, last:last + 1], sn[:, hh * 256:(hh + 1) * 256], ALU.mult, ALU.add,
                                [t_S32[d], t_E1, t_sn], [t_S32[d]])
                        if isx:
                            AM, t_AM = am.next()
                            tt(AM[:], ba[:, 256:512].rearrange("p (h t) -> p h t", h=2), Mmask2, ALU.mult, [t_ba, t_cst], [t_AM])
                            st.update(AM=AM, t_AM=t_AM)
                    elif j == 7:
                        if isx:
                            QD, t_QD, AM, t_AM = st["QD"], st["t_QD"], st["AM"], st["t_AM"]
                            sbp, t_sbp = sbcur[d]
                            for hh in range(2):
                                for jj in range(2):
                                    cc = hh * 2 + jj
                                    mm(ob[:, cc * 128:(cc + 1) * 128], vt2[:, b, cc * 128:(cc + 1) * 128], AM[:, hh, :], True, False,
                                       [t_vt, t_AM], [t_ob])
                                    mm(ob[:, cc * 128:(cc + 1) * 128], sbp[:, hh, jj * 128:(jj + 1) * 128], QD[:, hh, :], False, True,
                                       [t_sbp, t_QD], [t_ob])
                        nsb, t_nsb = Sbf[d].next()
                        P.add("act", lambda e, nsb=nsb, d=d: e.copy(nsb[:], S32[d][:]), reads=[t_S32[d]], writes=[t_nsb])
                        sbcur[d] = (nsb, t_nsb)
                    elif j == 8:
                        if isx:
                            src = ob[:, :].rearrange("p (c t) -> p c t", c=4)
                            if b not in visited:
                                visited.add(b)
                                P.add("act", lambda e, src=src, tok=tok: e.copy(OT2[:, :, tok], src), reads=[t_ob], writes=[t_OT[b]])
                            else:
                                P.add("dve", lambda e, src=src, tok=tok: e.tensor_tensor(OT2[:, :, tok], OT2[:, :, tok], src, ALU.add),
                                      reads=[t_ob, t_OT[b]], writes=[t_OT[b]])

                NST = 9
                for step in range(nsl + NST - 1):
                    for j in range(NST - 1, -1, -1):
                        i = step - j
                        if 0 <= i < nsl:
                            gstage(j, i)

                wog = []
                for cc in range(4):
                    wb, t_wb = W2K.next()
                    dma("sp", wb[:], wf_s[24 + 4 * hp + cc], reads=[t_wf[24 + 4 * hp + cc]], writes=[t_wb])
                    wog.append((wb, t_wb))
                for i in range(NTT):
                    ts_ = slice(i * 512, (i + 1) * 512)
                    t_oti = t_OT[4 * i:4 * i + 4]
                    for hh in range(2):
                        pst, t_pst = PS_
                        for jj in range(2):
                            sq, t_sqq = sqr.next()
                            act(sq[:], OT2[:, hh * 2 + jj, ts_], AF.Square, t_oti, [t_sqq])
                            mm(pst[:], ones_h[:], sq[:], jj == 0, jj == 1, [t_ones, t_sqq], [t_pst])
                        tmp, t_tmp = tf.next()
                        act(tmp[:], pst[:], AF.Sqrt, [t_pst, t_kc], [t_tmp], bias=eps_ap)
                        P.add("dve", lambda e, tmp=tmp: e.reciprocal(rstd_[:], tmp[:]), reads=[t_tmp], writes=[t_rstd])
                        for jj in range(2):
                            cc = hh * 2 + jj
                            wb, t_wb = wog[cc]
                            ps, t_ps = PA.next()
                            for kc in range(8):
                                mm(ps[:], wb[:, kc, :], U2T[:, kc, ts_], kc == 0, kc == 7, [t_wb, t_U2[i]], [t_ps])
                            sg, t_sg = tf.next()
                            act(sg[:], ps[:], AF.Silu, [t_ps], [t_sg])
                            t1, t_t1 = tf.next()
                            tt(t1[:], OT2[:, cc, ts_], rstd_[:], ALU.mult, t_oti + [t_rstd], [t_t1])
                            gcol = V_GN + 4 * hp + cc
                            stt(OT2[:, cc, ts_], t1[:], vecs[:, gcol:gcol + 1], sg[:], ALU.mult, ALU.mult,
                                [t_t1, t_vecs, t_sg], t_oti)
                dma("act", og_s[s, hp * 512:(hp + 1) * 512, :].rearrange("(j p) t -> p j t", p=128), OT2[:], reads=t_OT, writes=[t_ogs[hp]])

            phase_alias(SET_CONV, SET_FFN + SET_GLA)
            def conv_proj(c):
                zt, t_zt = ZT.next()
                P.add("dve", lambda e, zt=zt: e.memset(zt[:, 0:15], 0.0), writes=[t_zt])
                P.add("dve", lambda e, zt=zt: e.memset(zt[:, NT + 15:NT + 30], 0.0), writes=[t_zt])
                wa_, t_wa_ = W2K.next()
                dma("sp", wa_[:], wf_s[c], reads=[t_wf[c]], writes=[t_wa_])
                wb_, t_wb_ = W2K.next()
                dma("sp", wb_[:], wf_s[8 + c], reads=[t_wf[8 + c]], writes=[t_wb_])
                for i in range(NTT):
                    ts_ = slice(i * 512, (i + 1) * 512)
                    pa, t_pa = PA.next()
                    pb, t_pbb = PB.next()
                    for kc in range(8):
                        mm(pa[:], wa_[:, kc, :], U2T[:, kc, ts_], kc == 0, kc == 7, [t_wa_, t_U2[i]], [t_pa])
                    for kc in range(8):
                        mm(pb[:], wb_[:, kc, :], U2T[:, kc, ts_], kc == 0, kc == 7, [t_wb_, t_U2[i]], [t_pbb])
                    sg, t_sg = tf.next()
                    act(sg[:], pb[:], AF.Sigmoid, [t_pbb], [t_sg])
                    tt(zt[:, 15 + i * 512:15 + (i + 1) * 512], sg[:], pa[:], ALU.mult, [t_sg, t_pa], [t_zt])
                dg, t_dg = DG.next()
                for tap in range(31):
                    P.add("dve", lambda e, dg=dg, tap=tap, c=c: e.tensor_scalar(
                        dg[:, tap, :], ident_bf[:], vecs[:, V_DW + c * 31 + tap:V_DW + c * 31 + tap + 1], None, ALU.mult),
                        reads=[t_identbf, t_vecs], writes=[t_dg])
                return zt, t_zt, dg, t_dg

            def conv_mm(c, zt, t_zt, dg, t_dg):
                cvc, t_cvc = CVc.next()
                for i in range(NTT):
                    pd, t_pd = PD.next()
                    for tap in range(31):
                        mm(pd[:], dg[:, tap, :], zt[:, i * 512 + tap:i * 512 + tap + 512], tap == 0, tap == 30, [t_dg, t_zt], [t_pd])
                    act(cvc[:, i * 512:(i + 1) * 512], pd[:], AF.Identity, [t_pd, t_vecs], [t_cvc], bias=vecs[:, V_DWB + c:V_DWB + c + 1])
                dma("act", cv_s[s, c * 128:(c + 1) * 128, :], cvc[:], reads=[t_cvc], writes=[t_cvs[c]])

            cur_c = conv_proj(0)
            for c in range(8):
                nxt_c = conv_proj(c + 1) if c + 1 < 8 else None
                conv_mm(c, *cur_c)
                cur_c = nxt_c

            phase_alias(SET_FFN, SET_GLA + SET_CONV)

            def c_m1(i):
                ts_ = slice(i * 512, (i + 1) * 512)
                X, tX = XTs[i % 2], t_XTs[i % 2]
                pst, t_pst = PS_
                cl = []

                dma("sp", CVt[:], cv_s[s, :, ts_].rearrange("(c p) t -> p c t", p=128), reads=t_cvs, writes=[t_CVt])

                def ld_og():
                    dma("sp", OGt[:], og_s[s, :, ts_].rearrange("(c p) t -> p c t", p=128), reads=t_ogs, writes=[t_OGt])

                def ld_x():
                    dma("sp", X[:], x1_s[s, :, ts_].rearrange("(c p) t -> p c t", p=128), reads=[t_x1[i]], writes=[tX])

                def mean_mm(c):
                    def f():
                        mm(pst[:], ones_bf[:], CVt[:, c, :], c == 0, c == 7, [t_ones, t_CVt], [t_pst])
                        if c == 7:
                            P.add("act", lambda e: e.copy(mean_[:], pst[:]), reads=[t_pst], writes=[t_mean])
                    return f
                cl += [mean_mm(c) for c in range(8)]
                cl.append(ld_og)
                sqs = {}

                def sqf(c):
                    sq, t_sqq = tb.next()
                    act(sq[:], CVt[:, c, :], AF.Square, [t_CVt], [t_sqq])
                    sqs[c] = (sq, t_sqq)

                def ex2_mm(c):
                    def f():
                        if c == 0:
                            sqf(0)
                        if c + 1 < 8:
                            sqf(c + 1)
                        sq, t_sqq = sqs.pop(c)
                        mm(pst[:], ones_bf[:], sq[:], c == 0, c == 7, [t_ones, t_sqq], [t_pst])
                    return f
                cl += [ex2_mm(c) for c in range(8)]
                cl.append(ld_x)

                def var_chain():
                    m2, t_m2 = tf.next()
                    tt(m2[:], mean_[:], mean_[:], ALU.mult, [t_mean], [t_m2])
                    var, t_var = tf.next()
                    tt(var[:], pst[:], m2[:], ALU.subtract, [t_pst, t_m2], [t_var])
                    P.add("dve", lambda e, var=var: e.tensor_scalar(var[:], var[:], 0.0, None, ALU.max), reads=[t_var], writes=[t_var])
                    sd, t_sd = tf.next()
                    act(sd[:], var[:], AF.Sqrt, [t_var, t_kc], [t_sd], bias=eps_ap)
                    P.add("dve", lambda e, sd=sd: e.reciprocal(rstd_[:], sd[:]), reads=[t_sd], writes=[t_rstd])
                cl.append(var_chain)

                def nrm(c):
                    def f():
                        d1, t_d1 = tf.next()
                        tt(d1[:], CVt[:, c, :], mean_[:], ALU.subtract, [t_CVt, t_mean], [t_d1], eng="pool")
                        d2, t_d2 = tf.next()
                        tt(d2[:], d1[:], rstd_[:], ALU.mult, [t_d1, t_rstd], [t_d2])
                        act(NTb[:, c, :], d2[:], AF.Silu, [t_d2, t_vecs], [t_NTb], bias=vecs[:, V_LNB + c:V_LNB + c + 1],
                            scale=vecs[:, V_LNG + c:V_LNG + c + 1])
                    return f
                cl += [nrm(c) for c in range(8)]
                return cl

            def c_m7(i):
                X, tX = XTs[i % 2], t_XTs[i % 2]
                cl = stats_cl(X[:, :, :], [tX], 512, ones_bf[:])

                def scale(c):
                    def f():
                        stt(X[:, c, :], X[:, c, :], vecs[:, V_GFIN + c:V_GFIN + c + 1], rstd_[:], ALU.mult, ALU.mult,
                            [tX, t_vecs, t_rstd], [tX])
                    return f
                cl += [scale(c) for c in range(8)]

                def tr(sub, half):
                    def f():
                        ps, t_ps = PD.next()
                        for cc in range(4):
                            c = half * 4 + cc
                            P.add("pe", lambda e, ps=ps, cc=cc, c=c, sub=sub: e.transpose(
                                ps[:, cc * 128:(cc + 1) * 128], X[:, c, sub * 128:(sub + 1) * 128], ident), reads=[tX, t_cst], writes=[t_ps])
                        if half == 0:
                            act(xtok[:, sub, 0:512], ps[:], AF.Copy, [t_ps], [t_xtok])
                        else:
                            P.add("dve", lambda e, ps=ps, sub=sub: e.tensor_copy(xtok[:, sub, 512:1024], ps[:]), reads=[t_ps], writes=[t_xtok])
                    return f
                cl += [tr(sub, half) for sub in range(4) for half in range(2)]

                def store():
                    to = T(f"out_{s}_{i}")
                    r0 = s * NT + i * 512
                    dma("act", out_d[r0:r0 + 512, :].rearrange("(i p) d -> p i d", p=128), xtok[:], reads=[t_xtok], writes=[to])
                    t_out.append(to)
                cl.append(store)
                return cl

            for cl in c_m1(0):
                cl()
            for i in range(NTT):
                ts_ = slice(i * 512, (i + 1) * 512)
                X, tX = XTs[i % 2], t_XTs[i % 2]
                for o in range(8):
                    res = {}
                    for nm, wsrc, wt, widx, rhs3, t_rhs, rot_ in (
                            ("yc", sq_s[0], t_sq[0], o, NTb, [t_NTb], PA), ("ga", wf_s, t_wf, 32 + o, U2T[:, :, ts_], [t_U2[i]], PB),
                            ("yg", sq_s[1], t_sq[1], o, OGt, [t_OGt], PA), ("gb", wf_s, t_wf, 40 + o, U2T[:, :, ts_], [t_U2[i]], PB)):
                        wb, t_wb = W2K.next()
                        dma("sp", wb[:], wsrc[widx], reads=[wt[widx]], writes=[t_wb])
                        ps, t_ps = rot_.next()
                        for kc in range(8):
                            mm(ps[:], wb[:, kc, :], rhs3[:, kc, :], kc == 0, kc == 7, [t_wb] + t_rhs, [t_ps])
                        res[nm] = (ps, t_ps)
                        drain(qB, 1)
                    sga, t_sga = tf.next()
                    act(sga[:], res["ga"][0][:], AF.Sigmoid, [res["ga"][1]], [t_sga])
                    y1, t_y1 = tf.next()
                    tt(y1[:], sga[:], res["yc"][0][:], ALU.mult, [t_sga, res["yc"][1]], [t_y1])
                    sgb, t_sgb = tf.next()
                    act(sgb[:], res["gb"][0][:], AF.Sigmoid, [res["gb"][1]], [t_sgb])
                    y2, t_y2 = tf.next()
                    tt(y2[:], sgb[:], res["yg"][0][:], ALU.mult, [t_sgb, res["yg"][1]], [t_y2])
                    tt(MTt[:, o, :], y1[:], y2[:], ALU.add, [t_y1, t_y2], [t_MTt])
                flush(qB)
                for o in range(8):
                    wb, t_wb = W2K.next()
                    dma("sp", wb[:], sq_s[2][o], reads=[t_sq[2][o]], writes=[t_wb])
                    pd, t_pd = PD.next()
                    for kc in range(8):
                        mm(pd[:], wb[:, kc, :], MTt[:, kc, :], kc == 0, kc == 7, [t_wb, t_MTt], [t_pd])
                    stt(X[:, o, :], pd[:], SC[:, colx, 5, o:o + 1], X[:, o, :], ALU.mult, ALU.add, [t_pd, t_SC, tX], [tX])
                    if o == 0:
                        qM = deque(stats_cl(X[:, :, :], [tX], 512, ones_bf[:]) +
                                   modulate_cl(uF[:, :, :], t_uF, X[:, :, :], [tX], 512, SC[:, colx, 6, :], SC[:, colx, 7, :]))
                    else:
                        drain(qM, 1)
                nxt = c_m1(i + 1) if i + 1 < NTT else []
                flush(qM)
                qA.extend(nxt)
                gate_up(512, 1, qA, 1)
                down(512, 1, SC[:, colx, 8, :], X, tX, qA, 3)
                flush(qA)
                qB.extend(c_m7(i))
            flush(qB)

        P.add("sp", lambda e: None, reads=t_out)
        P.add("pool", lambda e: None, reads=t_out)

        with nc.Block() as block:
            def run_eng(ename):
                def f(eng):
                    P.emit_one(ename, eng, sems, dsems)
                return f
            block.tensor(run_eng("pe"))
            block.scalar(run_eng("act"))
            block.vector(run_eng("dve"))
            block.gpsimd(run_eng("pool"))
            block.sync(run_eng("sp"))
    return nc, P


def _consts():
    a = np.arange(128)
    cst = np.zeros((128, 7, 128), np.float32)
    cst[:, 0, :] = np.eye(128, dtype=np.float32)
    cst[:, 1, :] = (a[:, None] <= a[None, :])
    cst[:, 2, :] = cst[:, 1, :]
    cst[:, 3, :] = (a[:, None] >= a[None, :])
    cst[:, 4, :] = cst[:, 3, :]
    cst[:, 5, :] = (a[:, None] > a[None, :])
    cst[:, 6, :] = (a[:, None] < a[None, :])
    return cst


def _pp(v):
    return np.ascontiguousarray(np.asarray(v, np.float32).reshape(-1, 128).T)


def make_in_maps(inp, nseq, ncores):
    f = lambda k: np.asarray(inp[k], np.float32)
    x, c, ctx = f("x"), f("c"), f("ctx")
    NT, NCX = x.shape[1], ctx.shape[1]
    vecs = np.zeros((128, 384), np.float32)
    for off, k in ((0, "g_ffn1"), (8, "g_mix"), (16, "g_ffn2"), (24, "g_final"), (32, "dw_bias"), (40, "conv_ln_g"),
                   (48, "conv_ln_b"), (56, "gla_norm_g")):
        vecs[:, off:off + 8] = _pp(f(k).reshape(-1))
    vecs[:, 64:136] = _pp(f("b_mod").reshape(-1))
    dw = f("dw_weight").reshape(31, D)
    vecs[:, 136:384] = dw.T.reshape(8, 128, 31).transpose(1, 0, 2).reshape(128, 248)
    wal = np.zeros((64, 512), np.float32)
    wal[0:16] = f("w_alpha_f").reshape(16, 512)
    wal[16] = f("b_alpha_f").reshape(512)
    wal[32:48] = f("w_alpha_b").reshape(16, 512)
    wal[48] = f("b_alpha_b").reshape(512)
    cst = _consts()
    shared = {
        "vecs": vecs, "walpha": wal, "consts": cst,
        "w_mod": f("w_mod")[0], "w1_gu": f("w1_gu")[0], "w1_down": f("w1_down")[0], "w_in": f("w_in")[0],
        "w_conv_out": f("w_conv_out")[0], "w_gla_out": f("w_gla_out")[0], "w_out": f("w_out")[0],
        "w2_gu": f("w2_gu")[0], "w2_down": f("w2_down")[0],
    }
    maps = []
    for i in range(ncores):
        sl = slice(i * nseq, (i + 1) * nseq)
        cv = np.concatenate([c[sl], f("c_ctx")[None, :]], axis=0)
        cT = np.ascontiguousarray(cv.T.reshape(8, 128, nseq + 1).transpose(1, 0, 2))
        m = dict(shared)
        m["x"] = np.ascontiguousarray(x[sl].reshape(nseq * NT, D))
        m["ctx"] = np.ascontiguousarray(ctx[sl].reshape(nseq * NCX, D))
        m["cT"] = cT
        maps.append(m)
    return maps


def kernel(**inputs):
    x = inputs["x"]
    B, NT, _ = x.shape
    NCX = inputs["ctx"].shape[1]
    nseq = B // NCORES
    nc, _ = build(nseq, NT, NCX)
    maps = make_in_maps(inputs, nseq, NCORES)
    res = run_bass_kernel_spmd(nc, maps, core_ids=list(range(NCORES)))
    out = np.concatenate([r["out"].reshape(nseq, NT, D) for r in res.results], axis=0)
    return out.astype(np.float32)
```

```python
import numpy as np
from contextlib import ExitStack
import concourse.bass as bass
import concourse.mybir as mybir
from concourse.bass_utils import run_bass_kernel_spmd

F32 = mybir.dt.float32
BF16 = mybir.dt.bfloat16
AF = mybir.ActivationFunctionType
ALU = mybir.AluOpType

ENGS = ("pe", "act", "dve", "pool", "sp")
NDMASEM = 8

D = 1024
DFF = 2816
NJ = DFF // 128
DIN = 7200
NMOD = 9
EPS = 1e-6
NCORES = 8


class T:
    __slots__ = ("name", "w", "r", "psum")

    def __init__(self, name, psum=False):
        self.name = name
        self.w = None
        self.r = []
        self.psum = psum


class Op:
    __slots__ = ("eng", "fn", "idx", "waits", "inc", "dma", "sem", "semval", "vc", "cnt")

    def __init__(self, eng, fn, idx, dma):
        self.eng = eng
        self.fn = fn
        self.idx = idx
        self.dma = dma
        self.waits = []
        self.inc = False
        self.sem = None
        self.semval = 0
        self.vc = None
        self.cnt = 0


class Prog:
    def __init__(self):
        self.ops = {e: [] for e in ENGS}
        self.clock = {e: {} for e in ENGS}
        self.ndma = {e: 0 for e in ENGS}
        self.dma_ops = {e: [] for e in ENGS}
        self._prepared = False

    def add(self, eng, fn, reads=(), writes=(), dma=False):
        op = Op(eng, fn, len(self.ops[eng]), dma)
        clk = self.clock[eng]
        deps = []

        def need(o, kind):
            if o is None:
                return
            if o.eng == eng and not o.dma and not dma and eng != "pool":
                if eng == "pe" or kind == "waw":
                    return
            deps.append(o)

        for t in reads:
            need(t.w, "raw")
            if t.psum:
                for o in t.r:
                    if o.eng != eng:
                        need(o, "war")
        for t in writes:
            need(t.w, "waw")
            for o in t.r:
                need(o, "war")
        if dma:
            k = self.ndma[eng]
            self.ndma[eng] = k + 1
            op.sem = k % NDMASEM
            op.semval = 16 * (k // NDMASEM + 1)
            if k >= NDMASEM:
                deps.append(self.dma_ops[eng][k - NDMASEM])
            self.dma_ops[eng].append(op)
        best = {}
        for o in deps:
            key = (o.eng, "d", o.sem) if o.dma else (o.eng, "c")
            val = o.semval if o.dma else o.idx
            if key not in best or best[key][0] < val:
                best[key] = (val, o)
        for key, (val, o) in best.items():
            if clk.get(key, -1) >= val:
                continue
            op.waits.append(o)
            o.inc = True
            for k2, v2 in o.vc.items():
                if clk.get(k2, -1) < v2:
                    clk[k2] = v2
            clk[key] = val
        op.vc = dict(clk)
        for t in writes:
            t.w = op
            t.r = []
        for t in reads:
            t.r.append(op)
        self.ops[eng].append(op)
        return op

    def prepare(self):
        for e in ENGS:
            c = 0
            for o in self.ops[e]:
                if not o.dma and o.inc:
                    c += 1
                    o.cnt = c

    def emit_one(self, e, eng, sems, dsems):
        if not self._prepared:
            self.prepare()
            self._prepared = True
        for o in self.ops[e]:
            best = {}
            for d in o.waits:
                if d.dma:
                    key = ("d", d.eng, d.sem)
                    v = d.semval
                else:
                    key = ("c", d.eng)
                    v = d.cnt
                if best.get(key, -1) < v:
                    best[key] = v
            for key, v in best.items():
                if key[0] == "d":
                    eng.wait_ge(dsems[key[1]][key[2]], v)
                else:
                    eng.wait_ge(sems[key[1]], v)
            ins = o.fn(eng)
            if ins is None:
                continue
            if o.dma:
                ins.then_inc(dsems[e][o.sem], 16)
            elif o.inc:
                ins.then_inc(sems[e], 1)


class Rot:
    def __init__(self, items):
        self.items = items
        self.i = 0

    def next(self):
        it = self.items[self.i % len(self.items)]
        self.i += 1
        return it


def build(nseq, NT, NCX):
    assert NT % 512 == 0 and NCX % 128 == 0 and NCX <= 512
    NTT = NT // 512
    NXB = NT // 128
    NCB = NCX // 128
    TOK = NT + NCX
    NB = TOK // 128
    NCOL = nseq + 1
    nc = bass.Bass("TRN2", target_bir_lowering=False)

    def dram_in(name, shape, dt=F32):
        return nc.dram_tensor(name, shape, dt, kind="ExternalInput").ap()

    def dram_sc(name, shape, dt):
        return nc.dram_tensor(name, shape, dt, kind="Internal").ap()

    x_d = dram_in("x", [nseq * NT, D])
    ctx_d = dram_in("ctx", [nseq * NCX, D])
    cT_d = dram_in("cT", [128, 8, NCOL])
    vecs_d = dram_in("vecs", [128, 384])
    wal_d = dram_in("walpha", [64, 512])
    cst_d = dram_in("consts", [128, 7, 128])
    wmod_d = dram_in("w_mod", [D, NMOD * D])
    w1gu_d = dram_in("w1_gu", [D, 2 * DFF])
    w1d_d = dram_in("w1_down", [DFF, D])
    win_d = dram_in("w_in", [D, DIN])
    wco_d = dram_in("w_conv_out", [D, D])
    wgo_d = dram_in("w_gla_out", [D, D])
    wout_d = dram_in("w_out", [D, D])
    w2gu_d = dram_in("w2_gu", [D, 2 * DFF])
    w2d_d = dram_in("w2_down", [DFF, D])
    out_d = nc.dram_tensor("out", [nseq * NT, D], F32, kind="ExternalOutput").ap()

    gu_s = [dram_sc(f"gu_s{i}", [NJ, 128, 8, 256], BF16) for i in range(2)]
    dn_s = [dram_sc(f"dn_s{i}", [8, 128, NJ, 128], BF16) for i in range(2)]
    wf_s = dram_sc("wf_s", [48, 128, 8, 128], BF16)
    wkv_s = dram_sc("wkv_s", [2, 128, 8, 768], BF16)
    wa_s = dram_sc("wa_s", [128, 8, 32], BF16)
    sq_s = [dram_sc(f"sq_s{i}", [8, 128, 8, 128], BF16) for i in range(3)]
    x1_s = dram_sc("x1_s", [nseq, D, NT], F32)
    og_s = dram_sc("og_s", [nseq, D, NT], BF16)
    cv_s = dram_sc("cv_s", [nseq, D, NT], BF16)

    P = Prog()
    es = ExitStack()
    with es:
        cur = [16512]
        SB_END = 229376

        def nbytes(shape, dt):
            n = 1
            for v in shape[1:]:
                n *= v
            return ((n * (4 if dt == F32 else 2) + 63) // 64) * 64

        def sb(name, shape, dt, at=None):
            nb = nbytes(shape, dt)
            if at is None:
                off = cur[0]
                cur[0] += nb
                assert cur[0] <= SB_END, f"SBUF overflow at {name}: {cur[0]}"
            else:
                off = at[0]
                at[0] += nb
            return nc.alloc_sbuf_tensor_at(name, shape, dt, offset=off)

        def psum(name):
            return es.enter_context(nc.psum_tensor(name, [128, 512], F32))

        def rot(name, n, shape, dt, at=None):
            return Rot([(sb(f"{name}{i}", shape, dt, at), T(f"{name}{i}")) for i in range(n)])

        def rot_ts(r):
            return [t for (_, t) in r.items]

        cst = sb("cst", [128, 7, 128], F32); t_cst = T("cst")
        ident = cst[:, 0, :]
        M_le, M_ge, M_gt, M_lt = cst[:, 1, :], cst[:, 3, :], cst[:, 5, :], cst[:, 6, :]
        M_le2, M_ge2 = cst[:, 1:3, :], cst[:, 3:5, :]
        ident_bf = sb("ident_bf", [128, 128], BF16); t_identbf = T("identbf")
        ones_d = sb("ones_d", [128, 128], F32)
        ones_h = sb("ones_h", [128, 128], BF16)
        ones_bf = sb("ones_bf", [128, 128], BF16)
        t_ones = T("ones")
        kc_ = sb("kconst", [128, 2], F32); t_kc = T("kconst")
        vecs = sb("vecs", [128, 384], F32); t_vecs = T("vecs")
        wal = sb("wal", [64, 512], F32); t_wal = T("wal")
        modT = sb("modT", [128, 72, NCOL], F32); t_modT = T("modT")
        SC = sb("SC", [128, NCOL, 9, 8], F32); t_SC = T("SC")
        scT = sb("scT", [128, 8, NCOL], F32); t_scT = T("scT")
        V_GF1, V_GMIX, V_GF2, V_GFIN, V_DWB, V_LNG, V_LNB, V_GN, V_BMOD, V_DW = 0, 8, 16, 24, 32, 40, 48, 56, 64, 136

        pbank = [psum(f"pb{i}") for i in range(8)]
        t_pb = [T(f"pb{i}", psum=True) for i in range(8)]
        PA = Rot([(pbank[0], t_pb[0]), (pbank[1], t_pb[1])])
        PB = Rot([(pbank[2], t_pb[2]), (pbank[3], t_pb[3])])
        PD = Rot([(pbank[4], t_pb[4]), (pbank[5], t_pb[5])])
        PS_ = (pbank[6], t_pb[6])
        PT = (pbank[7], t_pb[7])

        U2T = sb("U2T", [128, 8, TOK], BF16)
        t_U2 = [T(f"U2_{i}") for i in range(NTT + 1)]
        def u2_tile_of_block(b):
            return t_U2[b // 4] if b < NXB else t_U2[NTT]

        tf = rot("tf", 4, [128, 512], F32)
        sqr = rot("sqr", 3, [128, 512], BF16)
        tb = rot("tb", 2, [128, 512], BF16)
        rstd_ = sb("rstd", [128, 512], F32); t_rstd = T("rstd")
        mean_ = sb("mean", [128, 512], F32); t_mean = T("mean")
        W2K = rot("w2k", 4, [128, 8, 128], BF16)
        W4K = rot("w4k", 3, [128, 8, 256], BF16)
        WDN = rot("wdn", 2, [128, NJ, 128], BF16)
        WA = sb("wa", [128, 8, 32], BF16); t_WA = T("wa")

        arena0 = cur[0]
        p = [arena0]
        xtok = sb("xtok", [128, 4, D], F32, p); t_xtok = T("xtok")
        XTs = [sb(f"XT{k}", [128, 8, 512], F32, p) for k in range(2)]; t_XTs = [T(f"XT{k}") for k in range(2)]
        uF = sb("uF", [128, 8, 512], BF16, p); t_uF = [T(f"uF{c}") for c in range(8)]
        hF = sb("hF", [128, NJ, 512], BF16, p); t_hF = [T(f"hF{j}") for j in range(NJ)]
        CVt = sb("CVt", [128, 8, 512], BF16, p); t_CVt = T("CVt")
        OGt = sb("OGt", [128, 8, 512], BF16, p); t_OGt = T("OGt")
        NTb = sb("NTb", [128, 8, 512], BF16, p); t_NTb = T("NTb")
        MTt = sb("MTt", [128, 8, 512], BF16, p); t_MTt = T("MTt")
        end_ffn = p[0]
        SET_FFN = [t_xtok] + t_XTs + t_uF + t_hF + [t_CVt, t_OGt, t_NTb, t_MTt]
        p = [arena0]
        lrT = sb("lrT", [64, TOK], F32, p); t_lrT = T("lrT")
        WKV = rot("wkv", 1, [128, 8, 768], BF16, p)
        qT2 = sb("qT2", [128, 2, NT], BF16, p); t_qT = T("qT")
        kT2 = sb("kT2", [128, 2, TOK], BF16, p); t_kT = T("kT")
        kt2 = sb("kt2", [128, NB, 256], BF16, p); t_kt = T("kt")
        vt2 = sb("vt2", [128, NB, 512], BF16, p); t_vt = T("vt")
        OT2 = sb("OT2", [128, 4, NT], BF16, p); t_OT = [T(f"OT{i}") for i in range(NXB)]
        S32 = [sb(f"S32_{d}", [128, 2, 256], F32, p) for d in range(2)]; t_S32 = [T(f"S32_{d}") for d in range(2)]
        Sbf = [rot(f"Sbf{d}_", 2, [128, 2, 256], BF16, p) for d in range(2)]
        g32 = rot("g32", 2, [128, 256], F32, p)
        ge = rot("ge", 2, [128, 256], F32, p)
        e1 = rot("e1", 4, [128, 2, 128], F32, p)
        e2 = rot("e2", 2, [128, 2, 128], F32, p)
        e3 = rot("e3", 2, [128, 256], F32, p)
        qd = rot("qd", 4, [128, 2, 128], BF16, p)
        kd = rot("kd", 2, [128, 2, 128], BF16, p)
        kh = rot("kh", 2, [128, 256], BF16, p)
        am = rot("am", 2, [128, 2, 128], BF16, p)
        end_gla = p[0]
        SET_GLA = [t_lrT, t_qT, t_kT, t_kt, t_vt] + t_OT + t_S32 + rot_ts(WKV)
        for r_ in (Sbf[0], Sbf[1], g32, ge, e1, e2, e3, qd, kd, kh, am):
            SET_GLA += rot_ts(r_)
        p = [arena0]
        ZT = rot("ZT", 2, [128, NT + 30], BF16, p)
        DG = rot("DG", 2, [128, 31, 128], BF16, p)
        CVc = rot("CVc", 2, [128, NT], BF16, p)
        end_conv = p[0]
        SET_CONV = rot_ts(ZT) + rot_ts(DG) + rot_ts(CVc)
        cur[0] = max(end_ffn, end_gla, end_conv)
        assert cur[0] <= SB_END, f"SBUF overflow: {cur[0]}"
        build.sbuf_report = dict(arena0=arena0, end_ffn=end_ffn, end_gla=end_gla, end_conv=end_conv, top=cur[0], limit=SB_END)

        def phase_alias(new_ts, old_ts):
            ops_ = []
            seen = set()
            for t in old_ts:
                for o in ([t.w] if t.w is not None else []) + t.r:
                    if id(o) not in seen:
                        seen.add(id(o))
                        ops_.append(o)
            for t in new_ts:
                t.r.extend(ops_)

        sems = {e: es.enter_context(nc.semaphore(f"s_{e}")) for e in ENGS}
        dsems = {e: [es.enter_context(nc.semaphore(f"d_{e}{i}")) for i in range(NDMASEM)] for e in ENGS}

        def dma(q, out, in_, reads=(), writes=(), **kw):
            return P.add(q, lambda e: e.dma_start(out=out, in_=in_, **kw), reads=reads, writes=writes, dma=True)

        def mm(out, lhsT, rhs, start, stop, reads, writes):
            return P.add("pe", lambda e: e.matmul(out, lhsT, rhs, start=start, stop=stop), reads=reads, writes=writes)

        def act(out, in_, func, reads, writes, bias=None, scale=None):
            kw = {}
            if bias is not None:
                kw["bias"] = bias
            if scale is not None:
                kw["scale"] = scale
            return P.add("act", lambda e: e.activation(out, in_, func, **kw), reads=reads, writes=writes)

        def tt(out, in0, in1, op, reads, writes, eng="dve"):
            return P.add(eng, lambda e: e.tensor_tensor(out, in0, in1, op), reads=reads, writes=writes)

        def stt(out, in0, scalar, in1, op0, op1, reads, writes):
            return P.add("dve", lambda e: e.scalar_tensor_tensor(out=out, in0=in0, scalar=scalar, in1=in1, op0=op0, op1=op1),
                         reads=reads, writes=writes)

        dma("sp", cst[:], cst_d, writes=[t_cst])
        dma("sp", vecs[:], vecs_d, writes=[t_vecs])
        dma("sp", wal[:], wal_d, writes=[t_wal])
        dma("sp", scT[:], cT_d, writes=[t_scT])
        P.add("dve", lambda e: e.memset(ones_d[:], 1.0 / D), writes=[t_ones])
        P.add("dve", lambda e: e.memset(ones_h[:], 1.0 / 256), writes=[t_ones])
        P.add("dve", lambda e: e.memset(ones_bf[:], 1.0 / D), writes=[t_ones])
        P.add("dve", lambda e: e.memset(kc_[:, 0:1], EPS), writes=[t_kc])
        P.add("dve", lambda e: e.memset(kc_[:, 1:2], 1.0), writes=[t_kc])
        P.add("dve", lambda e: e.tensor_copy(ident_bf[:], ident), reads=[t_cst], writes=[t_identbf])
        eps_ap = kc_[:, 0:1]
        one_ap = kc_[:, 1:2]

        t_gu = [[T(f"gu{i}_{j}") for j in range(NJ)] for i in range(2)]
        t_dn = [[T(f"dn{i}_{o}") for o in range(8)] for i in range(2)]
        t_wf = [T(f"wf{i}") for i in range(48)]
        t_wkv = [T(f"wkv{h}") for h in range(2)]
        t_was = T("was")
        t_sq = [[T(f"sq{i}_{o}") for o in range(8)] for i in range(3)]

        def cast_gu2(i, src):
            for j in range(NJ):
                ta = T("tmp")
                dma("pool", gu_s[i][j, :, :, 0:128], src[:, j * 128:(j + 1) * 128].rearrange("(kc p) c -> p kc c", p=128), writes=[ta])
                dma("pool", gu_s[i][j, :, :, 128:256], src[:, DFF + j * 128:DFF + (j + 1) * 128].rearrange("(kc p) c -> p kc c", p=128),
                    reads=[ta], writes=[t_gu[i][j]])

        def cast_dn(i, src):
            for o in range(8):
                dma("pool", dn_s[i][o], src[:, o * 128:(o + 1) * 128].rearrange("(j p) c -> p j c", p=128), writes=[t_dn[i][o]])

        def cast_sq(i, src):
            for o in range(8):
                dma("pool", sq_s[i][o], src[:, o * 128:(o + 1) * 128].rearrange("(kc p) c -> p kc c", p=128), writes=[t_sq[i][o]])

        def cast_wf(idx, col0):
            dma("pool", wf_s[idx], win_d[:, col0:col0 + 128].rearrange("(kc p) c -> p kc c", p=128), writes=[t_wf[idx]])

        cast_gu2(0, w1gu_d)
        cast_dn(0, w1d_d)
        dma("pool", wa_s, win_d[:, 5120:5152].rearrange("(kc p) c -> p kc c", p=128), writes=[t_was])
        for h in range(4):
            cast_wf(16 + h, 2048 + h * 128)
            cast_wf(20 + h, 2560 + h * 128)
            if h % 2 == 0:
                hp_ = h // 2
                ta = T("tmp")
                dma("pool", wkv_s[hp_, :, :, 0:256], win_d[:, 2560 + hp_ * 256:2560 + (hp_ + 1) * 256].rearrange("(kc p) c -> p kc c", p=128), writes=[ta])
                dma("pool", wkv_s[hp_, :, :, 256:768], win_d[:, 3072 + hp_ * 512:3072 + (hp_ + 1) * 512].rearrange("(kc p) c -> p kc c", p=128),
                    reads=[ta], writes=[t_wkv[hp_]])
            cast_wf(24 + 2 * h, 4096 + (2 * h) * 128)
            cast_wf(24 + 2 * h + 1, 4096 + (2 * h + 1) * 128)
        def late_casts():
            for c in range(8):
                cast_wf(c, c * 128)
                cast_wf(8 + c, 1024 + c * 128)
            cast_sq(0, wco_d)
            cast_sq(1, wgo_d)
            for c in range(8):
                cast_wf(32 + c, 5152 + c * 128)
                cast_wf(40 + c, 6176 + c * 128)
            cast_sq(2, wout_d)
            cast_gu2(1, w2gu_d)
            cast_dn(1, w2d_d)

        P.add("act", lambda e: e.activation(scT[:], scT[:], AF.Silu), reads=[t_scT], writes=[t_scT])
        wm_bufs = [(xtok[:].rearrange("p a (b c) -> p (a b) c", c=512), t_xtok), (XTs[0][:], t_XTs[0]), (XTs[1][:], t_XTs[1])]
        pmod, t_pmod = PS_
        for ch in range(18):
            wm_view, t_wm = wm_bufs[ch % 3]
            dma("sp", wm_view, wmod_d[:, ch * 512:(ch + 1) * 512].rearrange("(kc p) c -> p kc c", p=128), writes=[t_wm])
            pr, t_pr = PA.next()
            for kc in range(8):
                mm(pr[0:NCOL, :], scT[:, kc, :], wm_view[:, kc, :], kc == 0, kc == 7, [t_wm, t_scT], [t_pr])
            rowt, t_rowt = tf.next()
            act(rowt[0:NCOL, :], pr[0:NCOL, :], AF.Copy, [t_pr], [t_rowt])
            for sub in range(4):
                j = ch * 4 + sub
                P.add("pe", lambda e, j=j, sub=sub, rowt=rowt: e.transpose(pmod[:, j * NCOL:(j + 1) * NCOL], rowt[0:NCOL, sub * 128:(sub + 1) * 128],
                                                                         ident[0:NCOL, 0:NCOL]), reads=[t_rowt, t_cst], writes=[t_pmod])
        for col in range(NCOL):
            P.add("dve", lambda e, col=col: e.tensor_tensor(
                modT[:, :, col], pmod[:, 0:72 * NCOL].rearrange("p (j c) -> p j c", c=NCOL)[:, :, col],
                vecs[:, V_BMOD:V_BMOD + 72], ALU.add), reads=[t_pmod, t_vecs], writes=[t_modT])
        for col in range(NCOL):
            for n, gv in enumerate((V_GF1, V_GMIX, V_GF2)):
                sh = modT[:, (3 * n) * 8:(3 * n) * 8 + 8, col]
                scl = modT[:, (3 * n + 1) * 8:(3 * n + 1) * 8 + 8, col]
                gt = modT[:, (3 * n + 2) * 8:(3 * n + 2) * 8 + 8, col]
                P.add("dve", lambda e, col=col, n=n, scl=scl, gv=gv: e.scalar_tensor_tensor(
                    out=SC[:, col, 3 * n, :], in0=scl, scalar=1.0, in1=vecs[:, gv:gv + 8], op0=ALU.add, op1=ALU.mult),
                    reads=[t_modT, t_vecs], writes=[t_SC])
                P.add("dve", lambda e, col=col, n=n, sh=sh: e.tensor_copy(SC[:, col, 3 * n + 1, :], sh), reads=[t_modT], writes=[t_SC])
                P.add("dve", lambda e, col=col, n=n, gt=gt: e.tensor_scalar(
                    SC[:, col, 3 * n + 2, :], gt, (1.0 if n == 1 else 0.5), None, ALU.mult), reads=[t_modT], writes=[t_SC])

        from collections import deque
        qA = deque()
        qB = deque()

        def drain(q, k):
            for _ in range(k):
                if q:
                    q.popleft()()

        def flush(q):
            while q:
                q.popleft()()

        def stats_cl(src3, t_src, W, ones_ap, nchunk=8):
            pst, t_pst = PS_
            sqs = {}

            def sqf(c):
                sq, t_sqq = sqr.next()
                act(sq[:, :W], src3[:, c, :], AF.Square, t_src, [t_sqq])
                sqs[c] = (sq, t_sqq)

            def mk(c):
                def f():
                    if c == 0:
                        sqf(0)
                    if c + 1 < nchunk:
                        sqf(c + 1)
                    sq, t_sqq = sqs.pop(c)
                    mm(pst[:, :W], ones_ap, sq[:, :W], c == 0, c == nchunk - 1, [t_ones, t_sqq], [t_pst])
                return f

            def fin():
                tmp, t_tmp = tf.next()
                act(tmp[:, :W], pst[:, :W], AF.Sqrt, [t_pst, t_kc], [t_tmp], bias=eps_ap)
                P.add("dve", lambda e: e.reciprocal(rstd_[:, :W], tmp[:, :W]), reads=[t_tmp], writes=[t_rstd])
            return [mk(c) for c in range(nchunk)] + [fin]

        def modulate_cl(dst3, t_dst, src3, t_src, W, A, Bv):
            def mk(c):
                def f():
                    tmp, t_tmp = tf.next()
                    stt(tmp[:, :W], src3[:, c, :], A[:, c:c + 1], rstd_[:, :W], ALU.mult, ALU.mult, t_src + [t_SC, t_rstd], [t_tmp])
                    act(dst3[:, c, :], tmp[:, :W], AF.Identity, [t_tmp, t_SC], [t_dst[c] if isinstance(t_dst, list) else t_dst],
                        bias=Bv[:, c:c + 1])
                return f
            return [mk(c) for c in range(8)]

        def gate_up(W, wi, q=None, per=1):
            for j in range(NJ):
                wb, t_wb = W4K.next()
                dma("sp", wb[:], gu_s[wi][j], reads=[t_gu[wi][j]], writes=[t_wb])
                pa, t_pa = PA.next()
                pb, t_pbb = PB.next()
                for kc in range(8):
                    mm(pa[:, :W], wb[:, kc, 0:128], uF[:, kc, :W], kc == 0, kc == 7, [t_wb, t_uF[kc]], [t_pa])
                if q is not None:
                    drain(q, per)
                for kc in range(8):
                    mm(pb[:, :W], wb[:, kc, 128:256], uF[:, kc, :W], kc == 0, kc == 7, [t_wb, t_uF[kc]], [t_pbb])
                if q is not None:
                    drain(q, per)
                sa, t_sa = tf.next()
                act(sa[:, :W], pa[:, :W], AF.Silu, [t_pa], [t_sa])
                tt(hF[:, j, :W], sa[:, :W], pb[:, :W], ALU.mult, [t_sa, t_pbb], [t_hF[j]])

        def down(W, wi, G, X, t_X, q=None, per=3):
            for o in range(8):
                wb, t_wb = WDN.next()
                dma("sp", wb[:], dn_s[wi][o], reads=[t_dn[wi][o]], writes=[t_wb])
                pd, t_pd = PD.next()
                for j in range(NJ):
                    mm(pd[:, :W], wb[:, j, :], hF[:, j, :W], j == 0, j == NJ - 1, [t_wb, t_hF[j]], [t_pd])
                stt(X[:, o, :W], pd[:, :W], G[:, o:o + 1], X[:, o, :W], ALU.mult, ALU.add, [t_pd, t_SC, t_X], [t_X])
                if q is not None:
                    drain(q, per)

        def load_tok(src_rows, W):
            nsub = W // 128
            dma("sp", xtok[:, 0:nsub, :], src_rows.rearrange("(i p) d -> p i d", p=128), writes=[t_xtok])

        def transpose_in(W, X, t_X):
            nsub = W // 128
            for c in range(8):
                ps, t_ps = PA.next()
                for i in range(nsub):
                    P.add("pe", lambda e, ps=ps, i=i, c=c: e.transpose(ps[:, i * 128:(i + 1) * 128], xtok[:, i, c * 128:(c + 1) * 128], ident),
                          reads=[t_xtok, t_cst], writes=[t_ps])
                act(X[:, c, :W], ps[:, :W], AF.Copy, [t_ps], [t_X])

        t_out = []
        for s in range(nseq):
            colx, colc = s, nseq
            t_x1 = [T(f"x1_{s}_{i}") for i in range(NTT)]
            t_ogs = [T(f"ogs_{s}_{h}") for h in range(2)]
            t_cvs = [T(f"cvs_{s}_{c}") for c in range(8)]

            def a_info(ti):
                isx = ti < NTT
                W = 512 if isx else NCX
                col = colx if isx else colc
                tok0 = ti * 512 if isx else NT
                src = x_d[s * NT + ti * 512: s * NT + ti * 512 + 512, :] if isx else ctx_d[s * NCX:(s + 1) * NCX, :]
                return isx, W, col, tok0, src

            def a_n1(ti):
                isx, W, col, tok0, src = a_info(ti)
                X, tX = XTs[ti % 2], t_XTs[ti % 2]
                return (stats_cl(X[:, :, :W], [tX], W, ones_bf[:]) +
                        modulate_cl(uF[:, :, :W], t_uF, X[:, :, :W], [tX], W, SC[:, col, 0, :], SC[:, col, 1, :]))

            def a_n2(ti):
                isx, W, col, tok0, src = a_info(ti)
                X, tX = XTs[ti % 2], t_XTs[ti % 2]
                return (stats_cl(X[:, :, :W], [tX], W, ones_bf[:]) +
                        modulate_cl(U2T[:, :, tok0:tok0 + W], t_U2[ti], X[:, :, :W], [tX], W, SC[:, col, 3, :], SC[:, col, 4, :]))

            isx, W, col, tok0, src = a_info(0)
            load_tok(src, W)
            transpose_in(W, XTs[0], t_XTs[0])
            for cl in a_n1(0):
                cl()
            for ti in range(NTT + 1):
                isx, W, col, tok0, src = a_info(ti)
                X, tX = XTs[ti % 2], t_XTs[ti % 2]
                if ti + 1 <= NTT:
                    isx1, W1, col1, tok1, src1 = a_info(ti + 1)
                    load_tok(src1, W1)
                gate_up(W, 0, qB, 1)
                flush(qB)
                if ti + 1 <= NTT:
                    transpose_in(W1, XTs[(ti + 1) % 2], t_XTs[(ti + 1) % 2])
                    qA.extend(a_n1(ti + 1))
                down(W, 0, SC[:, col, 2, :], X, tX, qA, 3)
                flush(qA)
                if isx:
                    dma("act", x1_s[s, :, ti * 512:(ti + 1) * 512].rearrange("(c p) t -> p c t", p=128), X[:], reads=[tX], writes=[t_x1[ti]])
                qB.extend(a_n2(ti))
            flush(qB)

            phase_alias(SET_GLA, SET_FFN)
            if s == 0:
                late_casts()
            P.add("dve", lambda e: e.memset(lrT[:], 1.0), writes=[t_lrT])
            dma("sp", WA[:], wa_s, reads=[t_was], writes=[t_WA])
            tiles = [(i * 512, 512, t_U2[i]) for i in range(NTT)] + [(NT, NCX, t_U2[NTT])]
            for (t0, W, tu) in tiles:
                ps, t_ps = PA.next()
                for half in range(2):
                    for kc in range(8):
                        mm(ps[32 * half:32 * half + 16, :W], WA[:, kc, 16 * half:16 * half + 16], U2T[:, kc, t0:t0 + W], kc == 0, kc == 7,
                           [t_WA, tu], [t_ps])
                for half in range(2):
                    act(lrT[32 * half:32 * half + 16, t0:t0 + W], ps[32 * half:32 * half + 16, :W], AF.Copy, [t_ps], [t_lrT])

            for hp in range(2):
                for hh in range(2):
                    h = 2 * hp + hh
                    wq, t_wq = W2K.next()
                    dma("sp", wq[:], wf_s[16 + h], reads=[t_wf[16 + h]], writes=[t_wq])
                    for i in range(NTT):
                        ps, t_ps = PA.next()
                        for kc in range(8):
                            mm(ps[:, :], wq[:, kc, :], U2T[:, kc, i * 512:(i + 1) * 512], kc == 0, kc == 7, [t_wq, t_U2[i]], [t_ps])
                        P.add("act", lambda e, i=i, ps=ps, hh=hh: e.mul(qT2[:, hh, i * 512:(i + 1) * 512], ps[:, :], float(128 ** -0.5)),
                              reads=[t_ps], writes=[t_qT])
                    wk, t_wk = W2K.next()
                    dma("sp", wk[:], wf_s[20 + h], reads=[t_wf[20 + h]], writes=[t_wk])
                    for (t0, W, tu) in tiles:
                        ps, t_ps = PB.next()
                        for kc in range(8):
                            mm(ps[:, :W], wk[:, kc, :], U2T[:, kc, t0:t0 + W], kc == 0, kc == 7, [t_wk, tu], [t_ps])
                        P.add("dve", lambda e, ps=ps, t0=t0, W=W, hh=hh: e.tensor_copy(kT2[:, hh, t0:t0 + W], ps[:, :W]), reads=[t_ps], writes=[t_kT])
                wv, t_wv = WKV.next()
                dma("sp", wv[:], wkv_s[hp], reads=[t_wkv[hp]], writes=[t_wv])
                for b in range(NB):
                    pk, t_pk = PA.next()
                    pv, t_pv = PD.next()
                    tu = u2_tile_of_block(b)
                    for kc in range(8):
                        mm(pk[:, 0:256], U2T[:, kc, b * 128:(b + 1) * 128], wv[:, kc, 0:256], kc == 0, kc == 7, [t_wv, tu], [t_pk])
                    for kc in range(8):
                        mm(pv[:, :], U2T[:, kc, b * 128:(b + 1) * 128], wv[:, kc, 256:768], kc == 0, kc == 7, [t_wv, tu], [t_pv])
                    P.add("dve", lambda e, pk=pk, b=b: e.tensor_copy(kt2[:, b, :], pk[:, 0:256]), reads=[t_pk], writes=[t_kt])
                    act(vt2[:, b, :], pv[:, :], AF.Copy, [t_pv], [t_vt])

                ctxb = list(range(NXB, NB))
                xb = list(range(NXB))
                ordf = ctxb + xb
                ordb = ctxb[::-1] + xb[::-1]
                slots = []
                for k in range(NB):
                    slots.append((0, ordf[k]))
                    slots.append((1, ordb[k]))
                nsl = len(slots)
                sbcur = []
                for d in range(2):
                    P.add("dve", lambda e, d=d: e.memset(S32[d][:], 0.0), writes=[t_S32[d]])
                    sb0, t_sb0 = Sbf[d].next()
                    P.add("dve", lambda e, sb0=sb0: e.memset(sb0[:], 0.0), writes=[t_sb0])
                    sbcur.append((sb0, t_sb0))
                visited = set()
                ST = [dict() for _ in range(nsl)]
                ZR = [(pbank[0], t_pb[0]), (pbank[1], t_pb[1])]
                BA = [(pbank[2], t_pb[2]), (pbank[3], t_pb[3])]
                SN = [(pbank[4], t_pb[4]), (pbank[5], t_pb[5])]
                OB = [(pbank[6], t_pb[6]), (pbank[7], t_pb[7])]

                def gstage(j, i, hp=hp):
                    d, b = slots[i]
                    st = ST[i]
                    base = 32 * d
                    Mcum = M_le if d == 0 else M_ge
                    Mrem = M_gt if d == 0 else M_lt
                    Mmask2 = M_le2 if d == 0 else M_ge2
                    last = 127 if d == 0 else 0
                    isx = b < NXB
                    tok = slice(b * 128, (b + 1) * 128)
                    zr, t_zr = ZR[i % 2]
                    ba, t_ba = BA[i % 2]
                    sn, t_sn = SN[i % 2]
                    ob, t_ob = OB[i % 2]
                    if j == 0:
                        mm(zr[:, 0:256], lrT[base:base + 17, tok], wal[base:base + 17, hp * 256:(hp + 1) * 256], True, True,
                           [t_lrT, t_wal], [t_zr])
                    elif j == 1:
                        gE, t_gE = ge.next()
                        act(gE[:], zr[:, 0:256], AF.Exp, [t_zr], [t_gE], scale=-1.0)
                        gS, t_gS = g32.next()
                        act(gS[:], gE[:], AF.Ln, [t_gE, t_kc], [t_gS], bias=one_ap)
                        st.update(gS=gS, t_gS=t_gS)
                    elif j == 2:
                        gS, t_gS = st["gS"], st["t_gS"]
                        mm(zr[:, 256:512], Mrem, gS[:], True, True, [t_gS, t_cst], [t_zr])
                        for hh in range(2):
                            mm(ba[:, hh * 128:(hh + 1) * 128], gS[:, hh * 128:(hh + 1) * 128], Mcum, True, True, [t_gS, t_cst], [t_ba])
                    elif j == 3:
                        E1, t_E1 = e1.next()
                        act(E1[:], ba[:, 0:256].rearrange("p (h t) -> p h t", h=2), AF.Exp, [t_ba], [t_E1], scale=-1.0 / 16)
                        E3, t_E3 = e3.next()
                        act(E3[:], zr[:, 256:512], AF.Exp, [t_zr], [t_E3], scale=-1.0 / 16)
                        st.update(E1=E1, t_E1=t_E1, E3=E3, t_E3=t_E3)
                        if isx:
                            E2, t_E2 = e2.next()
                            act(E2[:], ba[:, 0:256].rearrange("p (h t) -> p h t", h=2), AF.Exp, [t_ba], [t_E2], scale=1.0 / 16)
                            st.update(E2=E2, t_E2=t_E2)
                    elif j == 4:
                        E1, t_E1, E3, t_E3 = st["E1"], st["t_E1"], st["E3"], st["t_E3"]
                        KH, t_KH = kh.next()
                        tt(KH[:], kt2[:, b, :], E3[:], ALU.mult, [t_kt, t_E3], [t_KH])
                        st.update(KH=KH, t_KH=t_KH)
                        if isx:
                            E2, t_E2 = st["E2"], st["t_E2"]
                            QD, t_QD = qd.next()
                            qk_eng = "pool" if s > 0 else "dve"
                            tt(QD[:], qT2[:, :, tok], E1[:], ALU.mult, [t_qT, t_E1], [t_QD], eng=qk_eng)
                            KD, t_KD = kd.next()
                            tt(KD[:], kT2[:, :, tok], E2[:], ALU.mult, [t_kT, t_E2], [t_KD], eng=qk_eng)
                            st.update(QD=QD, t_QD=t_QD, KD=KD, t_KD=t_KD)
                    elif j == 5:
                        KH, t_KH = st["KH"], st["t_KH"]
                        for hh in range(2):
                            mm(sn[:, hh * 256:(hh + 1) * 256], KH[:, hh * 128:(hh + 1) * 128], vt2[:, b, hh * 256:(hh + 1) * 256], True, True,
                               [t_KH, t_vt], [t_sn])
                        if isx:
                            QD, t_QD, KD, t_KD = st["QD"], st["t_QD"], st["KD"], st["t_KD"]
                            for hh in range(2):
                                mm(ba[:, 256 + hh * 128:256 + (hh + 1) * 128], KD[:, hh, :], QD[:, hh, :], True, True, [t_KD, t_QD], [t_ba])
                    elif j == 6:
                        E1, t_E1 = st["E1"], st["t_E1"]
                        for hh in range(2):
                            stt(S32[d][:, hh, :], S32[d][:, hh, :], E1[:, hh, last:last + 1], sn[:, hh * 256:(hh + 1) * 256], ALU.mult, ALU.add,
                                [t_S32[d], t_E1, t_sn], [t_S32[d]])
                        if isx:
                            AM, t_AM = am.next()
                            tt(AM[:], ba[:, 256:512].rearrange("p (h t) -> p h t", h=2), Mmask2, ALU.mult, [t_ba, t_cst], [t_AM])
                            st.update(AM=AM, t_AM=t_AM)
                    elif j == 7:
                        if isx:
                            QD, t_QD, AM, t_AM = st["QD"], st["t_QD"], st["AM"], st["t_AM"]
                            sbp, t_sbp = sbcur[d]
                            for hh in range(2):
                                for jj in range(2):
                                    cc = hh * 2 + jj
                                    mm(ob[:, cc * 128:(cc + 1) * 128], vt2[:, b, cc * 128:(cc + 1) * 128], AM[:, hh, :], True, False,
                                       [t_vt, t_AM], [t_ob])
                                    mm(ob[:, cc * 128:(cc + 1) * 128], sbp[:, hh, jj * 128:(jj + 1) * 128], QD[:, hh, :], False, True,
                                       [t_sbp, t_QD], [t_ob])
                        nsb, t_nsb = Sbf[d].next()
                        P.add("act", lambda e, nsb=nsb, d=d: e.copy(nsb[:], S32[d][:]), reads=[t_S32[d]], writes=[t_nsb])
                        sbcur[d] = (nsb, t_nsb)
                    elif j == 8:
                        if isx:
                            src = ob[:, :].rearrange("p (c t) -> p c t", c=4)
                            if b not in visited:
                                visited.add(b)
                                P.add("act", lambda e, src=src, tok=tok: e.copy(OT2[:, :, tok], src), reads=[t_ob], writes=[t_OT[b]])
                            else:
                                P.add("dve", lambda e, src=src, tok=tok: e.tensor_tensor(OT2[:, :, tok], OT2[:, :, tok], src, ALU.add),
                                      reads=[t_ob, t_OT[b]], writes=[t_OT[b]])

                NST = 9
                for step in range(nsl + NST - 1):
                    for j in range(NST - 1, -1, -1):
                        i = step - j
                        if 0 <= i < nsl:
                            gstage(j, i)

                wog = []
                for cc in range(4):
                    wb, t_wb = W2K.next()
                    dma("sp", wb[:], wf_s[24 + 4 * hp + cc], reads=[t_wf[24 + 4 * hp + cc]], writes=[t_wb])
                    wog.append((wb, t_wb))
                for i in range(NTT):
                    ts_ = slice(i * 512, (i + 1) * 512)
                    t_oti = t_OT[4 * i:4 * i + 4]
                    for hh in range(2):
                        pst, t_pst = PS_
                        for jj in range(2):
                            sq, t_sqq = sqr.next()
                            act(sq[:], OT2[:, hh * 2 + jj, ts_], AF.Square, t_oti, [t_sqq])
                            mm(pst[:], ones_h[:], sq[:], jj == 0, jj == 1, [t_ones, t_sqq], [t_pst])
                        tmp, t_tmp = tf.next()
                        act(tmp[:], pst[:], AF.Sqrt, [t_pst, t_kc], [t_tmp], bias=eps_ap)
                        P.add("dve", lambda e, tmp=tmp: e.reciprocal(rstd_[:], tmp[:]), reads=[t_tmp], writes=[t_rstd])
                        for jj in range(2):
                            cc = hh * 2 + jj
                            wb, t_wb = wog[cc]
                            ps, t_ps = PA.next()
                            for kc in range(8):
                                mm(ps[:], wb[:, kc, :], U2T[:, kc, ts_], kc == 0, kc == 7, [t_wb, t_U2[i]], [t_ps])
                            sg, t_sg = tf.next()
                            act(sg[:], ps[:], AF.Silu, [t_ps], [t_sg])
                            t1, t_t1 = tf.next()
                            tt(t1[:], OT2[:, cc, ts_], rstd_[:], ALU.mult, t_oti + [t_rstd], [t_t1])
                            gcol = V_GN + 4 * hp + cc
                            stt(OT2[:, cc, ts_], t1[:], vecs[:, gcol:gcol + 1], sg[:], ALU.mult, ALU.mult,
                                [t_t1, t_vecs, t_sg], t_oti)
                dma("act", og_s[s, hp * 512:(hp + 1) * 512, :].rearrange("(j p) t -> p j t", p=128), OT2[:], reads=t_OT, writes=[t_ogs[hp]])

            phase_alias(SET_CONV, SET_FFN + SET_GLA)
            def conv_proj(c):
                zt, t_zt = ZT.next()
                P.add("dve", lambda e, zt=zt: e.memset(zt[:, 0:15], 0.0), writes=[t_zt])
                P.add("dve", lambda e, zt=zt: e.memset(zt[:, NT + 15:NT + 30], 0.0), writes=[t_zt])
                wa_, t_wa_ = W2K.next()
                dma("sp", wa_[:], wf_s[c], reads=[t_wf[c]], writes=[t_wa_])
                wb_, t_wb_ = W2K.next()
                dma("sp", wb_[:], wf_s[8 + c], reads=[t_wf[8 + c]], writes=[t_wb_])
                for i in range(NTT):
                    ts_ = slice(i * 512, (i + 1) * 512)
                    pa, t_pa = PA.next()
                    pb, t_pbb = PB.next()
                    for kc in range(8):
                        mm(pa[:], wa_[:, kc, :], U2T[:, kc, ts_], kc == 0, kc == 7, [t_wa_, t_U2[i]], [t_pa])
                    for kc in range(8):
                        mm(pb[:], wb_[:, kc, :], U2T[:, kc, ts_], kc == 0, kc == 7, [t_wb_, t_U2[i]], [t_pbb])
                    sg, t_sg = tf.next()
                    act(sg[:], pb[:], AF.Sigmoid, [t_pbb], [t_sg])
                    tt(zt[:, 15 + i * 512:15 + (i + 1) * 512], sg[:], pa[:], ALU.mult, [t_sg, t_pa], [t_zt])
                dg, t_dg = DG.next()
                for tap in range(31):
                    P.add("dve", lambda e, dg=dg, tap=tap, c=c: e.tensor_scalar(
                        dg[:, tap, :], ident_bf[:], vecs[:, V_DW + c * 31 + tap:V_DW + c * 31 + tap + 1], None, ALU.mult),
                        reads=[t_identbf, t_vecs], writes=[t_dg])
                return zt, t_zt, dg, t_dg

            def conv_mm(c, zt, t_zt, dg, t_dg):
                cvc, t_cvc = CVc.next()
                for i in range(NTT):
                    pd, t_pd = PD.next()
                    for tap in range(31):
                        mm(pd[:], dg[:, tap, :], zt[:, i * 512 + tap:i * 512 + tap + 512], tap == 0, tap == 30, [t_dg, t_zt], [t_pd])
                    act(cvc[:, i * 512:(i + 1) * 512], pd[:], AF.Identity, [t_pd, t_vecs], [t_cvc], bias=vecs[:, V_DWB + c:V_DWB + c + 1])
                dma("act", cv_s[s, c * 128:(c + 1) * 128, :], cvc[:], reads=[t_cvc], writes=[t_cvs[c]])

            cur_c = conv_proj(0)
            for c in range(8):
                nxt_c = conv_proj(c + 1) if c + 1 < 8 else None
                conv_mm(c, *cur_c)
                cur_c = nxt_c

            phase_alias(SET_FFN, SET_GLA + SET_CONV)

            def c_m1(i):
                ts_ = slice(i * 512, (i + 1) * 512)
                X, tX = XTs[i % 2], t_XTs[i % 2]
                pst, t_pst = PS_
                cl = []

                dma("sp", CVt[:], cv_s[s, :, ts_].rearrange("(c p) t -> p c t", p=128), reads=t_cvs, writes=[t_CVt])

                def ld_og():
                    dma("sp", OGt[:], og_s[s, :, ts_].rearrange("(c p) t -> p c t", p=128), reads=t_ogs, writes=[t_OGt])

                def ld_x():
                    dma("sp", X[:], x1_s[s, :, ts_].rearrange("(c p) t -> p c t", p=128), reads=[t_x1[i]], writes=[tX])

                def mean_mm(c):
                    def f():
                        mm(pst[:], ones_bf[:], CVt[:, c, :], c == 0, c == 7, [t_ones, t_CVt], [t_pst])
                        if c == 7:
                            P.add("act", lambda e: e.copy(mean_[:], pst[:]), reads=[t_pst], writes=[t_mean])
                    return f
                cl += [mean_mm(c) for c in range(8)]
                cl.append(ld_og)
                sqs = {}

                def sqf(c):
                    sq, t_sqq = tb.next()
                    act(sq[:], CVt[:, c, :], AF.Square, [t_CVt], [t_sqq])
                    sqs[c] = (sq, t_sqq)

                def ex2_mm(c):
                    def f():
                        if c == 0:
                            sqf(0)
                        if c + 1 < 8:
                            sqf(c + 1)
                        sq, t_sqq = sqs.pop(c)
                        mm(pst[:], ones_bf[:], sq[:], c == 0, c == 7, [t_ones, t_sqq], [t_pst])
                    return f
                cl += [ex2_mm(c) for c in range(8)]
                cl.append(ld_x)

                def var_chain():
                    m2, t_m2 = tf.next()
                    tt(m2[:], mean_[:], mean_[:], ALU.mult, [t_mean], [t_m2])
                    var, t_var = tf.next()
                    tt(var[:], pst[:], m2[:], ALU.subtract, [t_pst, t_m2], [t_var])
                    P.add("dve", lambda e, var=var: e.tensor_scalar(var[:], var[:], 0.0, None, ALU.max), reads=[t_var], writes=[t_var])
                    sd, t_sd = tf.next()
                    act(sd[:], var[:], AF.Sqrt, [t_var, t_kc], [t_sd], bias=eps_ap)
                    P.add("dve", lambda e, sd=sd: e.reciprocal(rstd_[:], sd[:]), reads=[t_sd], writes=[t_rstd])
                cl.append(var_chain)

                def nrm(c):
                    def f():
                        d1, t_d1 = tf.next()
                        tt(d1[:], CVt[:, c, :], mean_[:], ALU.subtract, [t_CVt, t_mean], [t_d1], eng="pool")
                        d2, t_d2 = tf.next()
                        tt(d2[:], d1[:], rstd_[:], ALU.mult, [t_d1, t_rstd], [t_d2])
                        act(NTb[:, c, :], d2[:], AF.Silu, [t_d2, t_vecs], [t_NTb], bias=vecs[:, V_LNB + c:V_LNB + c + 1],
                            scale=vecs[:, V_LNG + c:V_LNG + c + 1])
                    return f
                for c in range(8):
                    cl.append(nrm(c))
                    cl.append(lambda: None)
                return cl

            def c_m7(i):
                X, tX = XTs[i % 2], t_XTs[i % 2]
                cl = stats_cl(X[:, :, :], [tX], 512, ones_bf[:])

                def scale(c):
                    def f():
                        stt(X[:, c, :], X[:, c, :], vecs[:, V_GFIN + c:V_GFIN + c + 1], rstd_[:], ALU.mult, ALU.mult,
                            [tX, t_vecs, t_rstd], [tX])
                    return f
                cl += [scale(c) for c in range(8)]

                def tr(sub, half):
                    def f():
                        ps, t_ps = PD.next()
                        for cc in range(4):
                            c = half * 4 + cc
                            P.add("pe", lambda e, ps=ps, cc=cc, c=c, sub=sub: e.transpose(
                                ps[:, cc * 128:(cc + 1) * 128], X[:, c, sub * 128:(sub + 1) * 128], ident), reads=[tX, t_cst], writes=[t_ps])
                        if half == 0:
                            act(xtok[:, sub, 0:512], ps[:], AF.Copy, [t_ps], [t_xtok])
                        else:
                            P.add("dve", lambda e, ps=ps, sub=sub: e.tensor_copy(xtok[:, sub, 512:1024], ps[:]), reads=[t_ps], writes=[t_xtok])
                    return f
                cl += [tr(sub, half) for sub in range(4) for half in range(2)]

                def store():
                    to = T(f"out_{s}_{i}")
                    r0 = s * NT + i * 512
                    dma("act", out_d[r0:r0 + 512, :].rearrange("(i p) d -> p i d", p=128), xtok[:], reads=[t_xtok], writes=[to])
                    t_out.append(to)
                cl.append(store)
                return cl

            for cl in c_m1(0):
                cl()
            for i in range(NTT):
                ts_ = slice(i * 512, (i + 1) * 512)
                X, tX = XTs[i % 2], t_XTs[i % 2]
                for o in range(8):
                    res = {}
                    for nm, wsrc, wt, widx, rhs3, t_rhs, rot_ in (
                            ("yc", sq_s[0], t_sq[0], o, NTb, [t_NTb], PA), ("ga", wf_s, t_wf, 32 + o, U2T[:, :, ts_], [t_U2[i]], PB),
                            ("yg", sq_s[1], t_sq[1], o, OGt, [t_OGt], PA), ("gb", wf_s, t_wf, 40 + o, U2T[:, :, ts_], [t_U2[i]], PB)):
                        wb, t_wb = W2K.next()
                        dma("sp", wb[:], wsrc[widx], reads=[wt[widx]], writes=[t_wb])
                        ps, t_ps = rot_.next()
                        for kc in range(8):
                            mm(ps[:], wb[:, kc, :], rhs3[:, kc, :], kc == 0, kc == 7, [t_wb] + t_rhs, [t_ps])
                        res[nm] = (ps, t_ps)
                        drain(qB, 1)
                    sga, t_sga = tf.next()
                    act(sga[:], res["ga"][0][:], AF.Sigmoid, [res["ga"][1]], [t_sga])
                    y1, t_y1 = tf.next()
                    tt(y1[:], sga[:], res["yc"][0][:], ALU.mult, [t_sga, res["yc"][1]], [t_y1])
                    sgb, t_sgb = tf.next()
                    act(sgb[:], res["gb"][0][:], AF.Sigmoid, [res["gb"][1]], [t_sgb])
                    y2, t_y2 = tf.next()
                    tt(y2[:], sgb[:], res["yg"][0][:], ALU.mult, [t_sgb, res["yg"][1]], [t_y2])
                    tt(MTt[:, o, :], y1[:], y2[:], ALU.add, [t_y1, t_y2], [t_MTt])
                flush(qB)
                for o in range(8):
                    wb, t_wb = W2K.next()
                    dma("sp", wb[:], sq_s[2][o], reads=[t_sq[2][o]], writes=[t_wb])
                    pd, t_pd = PD.next()
                    for kc in range(8):
                        mm(pd[:], wb[:, kc, :], MTt[:, kc, :], kc == 0, kc == 7, [t_wb, t_MTt], [t_pd])
                    stt(X[:, o, :], pd[:], SC[:, colx, 5, o:o + 1], X[:, o, :], ALU.mult, ALU.add, [t_pd, t_SC, tX], [tX])
                    if o == 0:
                        qM = deque(stats_cl(X[:, :, :], [tX], 512, ones_bf[:]) +
                                   modulate_cl(uF[:, :, :], t_uF, X[:, :, :], [tX], 512, SC[:, colx, 6, :], SC[:, colx, 7, :]))
                    else:
                        drain(qM, 1)
                nxt = c_m1(i + 1) if i + 1 < NTT else []
                flush(qM)
                qA.extend(nxt)
                gate_up(512, 1, qA, 1)
                down(512, 1, SC[:, colx, 8, :], X, tX, qA, 3)
                flush(qA)
                qB.extend(c_m7(i))
            flush(qB)

        P.add("sp", lambda e: None, reads=t_out)
        P.add("pool", lambda e: None, reads=t_out)

        with nc.Block() as block:
            def run_eng(ename):
                def f(eng):
                    P.emit_one(ename, eng, sems, dsems)
                return f
            block.tensor(run_eng("pe"))
            block.scalar(run_eng("act"))
            block.vector(run_eng("dve"))
            block.gpsimd(run_eng("pool"))
            block.sync(run_eng("sp"))
    return nc, P


def _consts():
    a = np.arange(128)
    cst = np.zeros((128, 7, 128), np.float32)
    cst[:, 0, :] = np.eye(128, dtype=np.float32)
    cst[:, 1, :] = (a[:, None] <= a[None, :])
    cst[:, 2, :] = cst[:, 1, :]
    cst[:, 3, :] = (a[:, None] >= a[None, :])
    cst[:, 4, :] = cst[:, 3, :]
    cst[:, 5, :] = (a[:, None] > a[None, :])
    cst[:, 6, :] = (a[:, None] < a[None, :])
    return cst


def _pp(v):
    return np.ascontiguousarray(np.asarray(v, np.float32).reshape(-1, 128).T)


def make_in_maps(inp, nseq, ncores):
    f = lambda k: np.asarray(inp[k], np.float32)
    x, c, ctx = f("x"), f("c"), f("ctx")
    NT, NCX = x.shape[1], ctx.shape[1]
    vecs = np.zeros((128, 384), np.float32)
    for off, k in ((0, "g_ffn1"), (8, "g_mix"), (16, "g_ffn2"), (24, "g_final"), (32, "dw_bias"), (40, "conv_ln_g"),
                   (48, "conv_ln_b"), (56, "gla_norm_g")):
        vecs[:, off:off + 8] = _pp(f(k).reshape(-1))
    vecs[:, 64:136] = _pp(f("b_mod").reshape(-1))
    dw = f("dw_weight").reshape(31, D)
    vecs[:, 136:384] = dw.T.reshape(8, 128, 31).transpose(1, 0, 2).reshape(128, 248)
    wal = np.zeros((64, 512), np.float32)
    wal[0:16] = f("w_alpha_f").reshape(16, 512)
    wal[16] = f("b_alpha_f").reshape(512)
    wal[32:48] = f("w_alpha_b").reshape(16, 512)
    wal[48] = f("b_alpha_b").reshape(512)
    cst = _consts()
    shared = {
        "vecs": vecs, "walpha": wal, "consts": cst,
        "w_mod": f("w_mod")[0], "w1_gu": f("w1_gu")[0], "w1_down": f("w1_down")[0], "w_in": f("w_in")[0],
        "w_conv_out": f("w_conv_out")[0], "w_gla_out": f("w_gla_out")[0], "w_out": f("w_out")[0],
        "w2_gu": f("w2_gu")[0], "w2_down": f("w2_down")[0],
    }
    maps = []
    for i in range(ncores):
        sl = slice(i * nseq, (i + 1) * nseq)
        cv = np.concatenate([c[sl], f("c_ctx")[None, :]], axis=0)
        cT = np.ascontiguousarray(cv.T.reshape(8, 128, nseq + 1).transpose(1, 0, 2))
        m = dict(shared)
        m["x"] = np.ascontiguousarray(x[sl].reshape(nseq * NT, D))
        m["ctx"] = np.ascontiguousarray(ctx[sl].reshape(nseq * NCX, D))
        m["cT"] = cT
        maps.append(m)
    return maps


def kernel(**inputs):
    x = inputs["x"]
    B, NT, _ = x.shape
    NCX = inputs["ctx"].shape[1]
    nseq = B // NCORES
    nc, _ = build(nseq, NT, NCX)
    maps = make_in_maps(inputs, nseq, NCORES)
    res = run_bass_kernel_spmd(nc, maps, core_ids=list(range(NCORES)))
    out = np.concatenate([r["out"].reshape(nseq, NT, D) for r in res.results], axis=0)
    return out.astype(np.float32)
```
